# Optimizing a Trainium2 kernel written in Bass

```python
import math
import jax, jax.numpy as jnp
from jax import lax
import numpy as np

D_MODEL = 2048
BATCH = 4
SEQ = 2048
DEPTH = 1
DEC_BATCH = 128
DEC_SEQ = 1
PAST_LEN = 16384
PAGE_SIZE = 128

GLA_HEADS = 4
GLA_DK = D_MODEL // 2 // GLA_HEADS
GLA_DV = D_MODEL // GLA_HEADS
GLA_KW = GLA_HEADS * GLA_DK
GLA_VW = GLA_HEADS * GLA_DV
GATE_RANK = 16
GATE_TEMP = 16.0
GLA_CHUNK = 64
POOL_WINDOWS = (2, 4, 8, 16)
POOL_GROUPS = 4
POOL_GW = D_MODEL // 8
POOL_W = POOL_GROUPS * POOL_GW
POOL_BUF = 15
D_FF = 128 * ((8 * D_MODEL // 3 + 127) // 128)
CONV_W = 3
PLE_DIM = 256
EPS = 1e-6
IN_SIZES = (GLA_KW, GLA_KW, GLA_VW, GLA_VW, GATE_RANK, POOL_W, D_MODEL, D_MODEL)
IN_WIDTH = GLA_KW * 2 + GLA_VW * 2 + GATE_RANK + POOL_W + D_MODEL * 2

kernel_name = 'gla_pool_convffn_hybrid_step'


def rmsnorm(x, g):
    xf = x.astype(jnp.float32)
    y = xf * lax.rsqrt(jnp.mean(xf * xf, axis=-1, keepdims=True) + EPS)
    return (y * g.astype(jnp.float32)).astype(x.dtype)


def split_cols(z):
    idx = [int(i) for i in np.cumsum(IN_SIZES)[:-1]]
    return jnp.split(z, idx, axis=-1)


def gla_scan(q, k, v, glog, s0):
    B, T, H, DK = q.shape
    DV = v.shape[-1]
    C = math.gcd(T, GLA_CHUNK)
    NC = T // C

    def to_chunks(a):
        return a.reshape(B, NC, C, H, a.shape[-1]).transpose(1, 0, 3, 2, 4)

    qc, kc, vc, gc = to_chunks(q), to_chunks(k), to_chunks(v), to_chunks(glog)
    causal = jnp.tril(jnp.ones((C, C), dtype=bool))

    def step(S, inp):
        qi, ki, vi, gi = inp
        b = jnp.cumsum(gi, axis=2)
        b_last = b[:, :, -1:, :]
        qb = qi * jnp.exp(b)
        kb = ki * jnp.exp(-b)
        att = jnp.where(causal, jnp.einsum('bhtd,bhsd->bhts', qb, kb), 0.0)
        o = jnp.einsum('bhtd,bhdv->bhtv', qb, S) + jnp.einsum('bhts,bhsv->bhtv', att, vi)
        kd = ki * jnp.exp(b_last - b)
        S = jnp.exp(b_last[:, :, 0, :])[..., None] * S + jnp.einsum('bhsd,bhsv->bhdv', kd, vi)
        return S, o

    S, oc = lax.scan(step, s0, (qc, kc, vc, gc))
    o = oc.transpose(1, 0, 3, 2, 4).reshape(B, T, H, DV)
    return o, S


def pool_mix(u, buf, start_pos):
    B, T, _ = u.shape
    uf = u.astype(jnp.float32)
    ext = jnp.concatenate([buf.astype(jnp.float32), uf], axis=1)
    cs = jnp.concatenate([jnp.zeros((B, 1, POOL_W), jnp.float32), jnp.cumsum(ext, axis=1)], axis=1)
    pos = start_pos + jnp.arange(T)
    means = []
    for g, w in enumerate(POOL_WINDOWS):
        lo, hi = g * POOL_GW, (g + 1) * POOL_GW
        s = cs[:, POOL_BUF + 1:POOL_BUF + 1 + T, lo:hi] - cs[:, POOL_BUF + 1 - w:POOL_BUF + 1 - w + T, lo:hi]
        cnt = jnp.minimum(pos + 1, w).astype(jnp.float32)[None, :, None]
        means.append(s / cnt)
    mix = jnp.concatenate(means, axis=-1) - uf
    return mix.astype(u.dtype), ext[:, -POOL_BUF:].astype(buf.dtype)


def causal_dwconv(u, buf, w, b):
    T = u.shape[1]
    ext = jnp.concatenate([buf.astype(u.dtype), u], axis=1)
    y = b
    for j in range(CONV_W):
        y = y + w[j] * ext[:, j:j + T]
    return y, ext[:, -(CONV_W - 1):].astype(buf.dtype)


def trunk_layer(h, pl, s_gla, s_pool, s_conv, start_pos, lw):
    B, T, _ = h.shape
    dt = h.dtype
    f32 = jnp.float32
    a = rmsnorm(h, lw['norm_mix'])
    z = a @ lw['w_in']
    q, k, v, r, glr, u, ga, gb = split_cols(z)
    zg = glr.astype(f32) @ lw['w_gate_up'].astype(f32) + lw['b_gate'].astype(f32)
    glog = (jax.nn.log_sigmoid(zg) / GATE_TEMP).reshape(B, T, GLA_HEADS, GLA_DK)
    qh = q.astype(f32).reshape(B, T, GLA_HEADS, GLA_DK) * (GLA_DK ** -0.5)
    kh = k.astype(f32).reshape(B, T, GLA_HEADS, GLA_DK)
    vh = v.astype(f32).reshape(B, T, GLA_HEADS, GLA_DV)
    o, s_gla_new = gla_scan(qh, kh, vh, glog, s_gla.astype(f32))
    o = o * lax.rsqrt(jnp.mean(o * o, axis=-1, keepdims=True) + EPS)
    o = o * lw['gla_norm'].astype(f32).reshape(GLA_HEADS, GLA_DV)
    o = o.reshape(B, T, GLA_VW).astype(dt) * jax.nn.silu(r)
    y_a = o @ lw['w_branch_a']
    pm, s_pool_new = pool_mix(u, s_pool, start_pos)
    pm = jnp.einsum('btgc,gcd->btgd', pm.reshape(B, T, POOL_GROUPS, POOL_GW), lw['w_pool'])
    pm = pm.reshape(B, T, POOL_W) * lw['pool_scale']
    y_b = pm @ lw['w_branch_b']
    m = jax.nn.sigmoid(ga) * y_a + jax.nn.sigmoid(gb) * y_b
    h = h + m @ lw['w_out']
    c = rmsnorm(h, lw['norm_ffn'])
    gate, up = jnp.split(c @ lw['w_up'], 2, axis=-1)
    gconv, s_conv_new = causal_dwconv(gate, s_conv, lw['conv_w'], lw['conv_b'])
    h = h + (jax.nn.silu(gconv) * up) @ lw['w_down']
    pg = jax.nn.sigmoid(rmsnorm(h, lw['norm_ple']) @ lw['w_ple_gate'])
    h = h + pg * (pl @ lw['w_ple'])
    return h, s_gla_new.astype(s_gla.dtype), s_pool_new, s_conv_new


def setup_inputs(seed: int = 0) -> dict:
    key = jax.random.key(seed)
    ks = jax.random.split(key, 32)

    def nrm(k, shape, scale):
        return jax.random.normal(k, shape, jnp.float32) * scale

    L = DEPTH
    return {
        'x_prompt': nrm(ks[0], (BATCH, SEQ, D_MODEL), 1.0),
        'x_sample': nrm(ks[1], (DEC_BATCH, DEC_SEQ, D_MODEL), 1.0),
        'state_gla': nrm(ks[2], (L, DEC_BATCH, GLA_HEADS, GLA_DK, GLA_DV), 1.0),
        'state_pool': nrm(ks[3], (L, DEC_BATCH, POOL_BUF, POOL_W), 1.0),
        'state_conv': nrm(ks[4], (L, DEC_BATCH, CONV_W - 1, D_FF), 1.0),
        'p_prompt': nrm(ks[5], (L, BATCH, SEQ, PLE_DIM), 1.0),
        'p_sample': nrm(ks[6], (L, DEC_BATCH, DEC_SEQ, PLE_DIM), 1.0),
        'norm_mix': 1.0 + nrm(ks[7], (L, D_MODEL), 0.02),
        'w_in': nrm(ks[8], (L, D_MODEL, IN_WIDTH), D_MODEL ** -0.5),
        'w_gate_up': nrm(ks[9], (L, GATE_RANK, GLA_KW), GATE_RANK ** -0.5),
        'b_gate': nrm(ks[10], (L, GLA_KW), 0.01),
        'gla_norm': 1.0 + nrm(ks[11], (L, GLA_VW), 0.02),
        'w_branch_a': nrm(ks[12], (L, GLA_VW, D_MODEL), GLA_VW ** -0.5),
        'w_pool': nrm(ks[13], (L, POOL_GROUPS, POOL_GW, POOL_GW), POOL_GW ** -0.5),
        'pool_scale': 1.0 + nrm(ks[14], (L, POOL_W), 0.02),
        'w_branch_b': nrm(ks[15], (L, POOL_W, D_MODEL), POOL_W ** -0.5),
        'w_out': nrm(ks[16], (L, D_MODEL, D_MODEL), D_MODEL ** -0.5),
        'norm_ffn': 1.0 + nrm(ks[17], (L, D_MODEL), 0.02),
        'w_up': nrm(ks[18], (L, D_MODEL, 2 * D_FF), D_MODEL ** -0.5),
        'conv_w': nrm(ks[19], (L, CONV_W, D_FF), CONV_W ** -0.5),
        'conv_b': nrm(ks[20], (L, D_FF), 0.01),
        'w_down': nrm(ks[21], (L, D_FF, D_MODEL), D_FF ** -0.5),
        'norm_ple': 1.0 + nrm(ks[22], (L, D_MODEL), 0.02),
        'w_ple_gate': nrm(ks[23], (L, D_MODEL, D_MODEL), D_MODEL ** -0.5),
        'w_ple': nrm(ks[24], (L, PLE_DIM, D_MODEL), PLE_DIM ** -0.5),
        'norm_final': 1.0 + nrm(ks[25], (D_MODEL,), 0.02),
    }


def reference(x_prompt, x_sample, state_gla, state_pool, state_conv, p_prompt, p_sample,
              norm_mix, w_in, w_gate_up, b_gate, gla_norm, w_branch_a, w_pool, pool_scale,
              w_branch_b, w_out, norm_ffn, w_up, conv_w, conv_b, w_down, norm_ple, w_ple_gate,
              w_ple, norm_final):
    hp, hs = x_prompt, x_sample
    bp = x_prompt.shape[0]
    gla_p, pool_p, conv_p, gla_s, pool_s, conv_s = [], [], [], [], [], []
    for i in range(DEPTH):
        lw = {
            'norm_mix': norm_mix[i], 'w_in': w_in[i], 'w_gate_up': w_gate_up[i], 'b_gate': b_gate[i],
            'gla_norm': gla_norm[i], 'w_branch_a': w_branch_a[i], 'w_pool': w_pool[i],
            'pool_scale': pool_scale[i], 'w_branch_b': w_branch_b[i], 'w_out': w_out[i],
            'norm_ffn': norm_ffn[i], 'w_up': w_up[i], 'conv_w': conv_w[i], 'conv_b': conv_b[i],
            'w_down': w_down[i], 'norm_ple': norm_ple[i], 'w_ple_gate': w_ple_gate[i], 'w_ple': w_ple[i],
        }
        z_gla = jnp.zeros((bp, GLA_HEADS, GLA_DK, GLA_DV), state_gla.dtype)
        z_pool = jnp.zeros((bp, POOL_BUF, POOL_W), state_pool.dtype)
        z_conv = jnp.zeros((bp, CONV_W - 1, D_FF), state_conv.dtype)
        hp, g1, p1, c1 = trunk_layer(hp, p_prompt[i], z_gla, z_pool, z_conv, 0, lw)
        hs, g2, p2, c2 = trunk_layer(hs, p_sample[i], state_gla[i], state_pool[i], state_conv[i], PAST_LEN, lw)
        gla_p.append(g1); pool_p.append(p1); conv_p.append(c1)
        gla_s.append(g2); pool_s.append(p2); conv_s.append(c2)
    y_prompt = rmsnorm(hp, norm_final)
    y_sample = rmsnorm(hs, norm_final)
    new_gla_prompt = jnp.stack(gla_p, axis=0)
    new_pool_prompt = jnp.stack(pool_p, axis=0)
    new_conv_prompt = jnp.stack(conv_p, axis=0)
    new_gla_sample = jnp.stack(gla_s, axis=0)
    new_pool_sample = jnp.stack(pool_s, axis=0)
    new_conv_sample = jnp.stack(conv_s, axis=0)
    return (y_prompt, y_sample, new_gla_prompt, new_pool_prompt, new_conv_prompt, new_gla_sample, new_pool_sample, new_conv_sample)
```

```python
import numpy as np
from contextlib import ExitStack
import concourse.bass as bass
import concourse.mybir as mybir
from concourse.bass_utils import run_bass_kernel_spmd

F32 = mybir.dt.float32
BF16 = mybir.dt.bfloat16
U8 = mybir.dt.uint8
AF = mybir.ActivationFunctionType
ALU = mybir.AluOpType

D = 2048
KC = 16
DFF = 5504
FC = 43
NH = 4
DK = 256
DV = 512
PLE = 256
OQ, OK_, OV, OR, OGLR, OU, OGA, OGB = 0, 1024, 2048, 4096, 6144, 6160, 7184, 9232
INW = 11280
EPS = 1e-6
HALO = 32
NPRE = 992
PRE_SUBS = [[128, 128, 128, 128], [128, 128, 128, 96]]
NSAMP = 16
SPP = 8
SEM_LIMIT = 30000
DEBUG = False
DEBUG_PASS = 0
RING_NB = 4
RING_ELEMS = 4096

C_ID, C_UC, C_MK, C_OH, C_CORR, C_GT, C_GN, C_PS, C_CW = 0, 128, 256, 384, 448, 512, 560, 576, 584
C_TOT = 584 + 172


class Res:
    __slots__ = ("name", "w", "r")

    def __init__(self, name="", init=None):
        self.name = name
        self.w = dict(init) if init else {}
        self.r = {}


class Eng:
    def __init__(self, name, h):
        self.name = name
        self.h = h
        self.sem = None
        self.count = 0
        self.known = {}
        self.pending = False


class FW:
    def __init__(self, nc):
        self.nc = nc
        self.stack = ExitStack()
        self.sems = {}
        self.hist = {}
        self.nsem = 0
        self.E = {}
        self.dry = False
        for name, h in (("pe", nc.tensor), ("act", nc.scalar), ("dve", nc.vector),
                        ("pool", nc.gpsimd), ("sp", nc.sync)):
            e = Eng(name, h)
            self.E[name] = e
            self._new_eng_sem(e)
        self.dma_pool = {}
        self.zclock = {}
        self.n_wait = 0
        self.n_ins = 0

    def new_sem(self, name):
        self.nsem += 1
        key = f"{name}_{self.nsem}"
        h = self.stack.enter_context(self.nc.semaphore(key))
        self.sems[key] = h
        return key

    def _new_eng_sem(self, e):
        e.sem = self.new_sem("s" + e.name)
        e.count = 0

    def make_dma_pool(self, qname, n):
        self.dma_pool[qname] = {"keys": [self.new_sem(f"d{qname}") for _ in range(n)],
                                "vals": [0] * n, "i": 0}

    def res(self, name=""):
        return Res(name, self.zclock)

    def new_phase(self):
        if self.dry:
            return
        clk = {}
        for e in self.E.values():
            if e.pending:
                raise RuntimeError("pending op at phase boundary " + e.name)
            if e.count > 0:
                clk[e.sem] = e.count
        for pool in self.dma_pool.values():
            for k, v in zip(pool["keys"], pool["vals"]):
                if v > 0:
                    clk[k] = v
        self.zclock = clk

    def _wait(self, e, need, own_raw):
        for key, val in need.items():
            if key == e.sem and not own_raw.get(key):
                continue
            if e.known.get(key, 0) >= val:
                continue
            e.h.wait_ge(self.sems[key], val)
            self.n_wait += 1
            snap = self.hist.get((key, val))
            if snap:
                for k2, v2 in snap.items():
                    if e.known.get(k2, 0) < v2:
                        e.known[k2] = v2
            e.known[key] = val

    def _deps(self, e, reads, writes):
        need = {}
        own_raw = {}
        for r in reads:
            for k, v in r.w.items():
                if need.get(k, 0) < v:
                    need[k] = v
                if k == e.sem:
                    own_raw[k] = True
        for w in writes:
            for d in (w.w, w.r):
                for k, v in d.items():
                    if need.get(k, 0) < v:
                        need[k] = v
        return need, own_raw

    def _record(self, key, val, reads, writes):
        for r in reads:
            if r.r.get(key, 0) < val:
                r.r[key] = val
        for w in writes:
            w.w = {key: val}
            w.r = {}

    def op(self, eng, fn, reads=(), writes=(), signal=True):
        if self.dry:
            return None
        e = self.E[eng]
        need, own_raw = self._deps(e, reads, writes)
        self._wait(e, need, own_raw)
        ins = fn(e.h)
        self.n_ins += 1
        if signal:
            if e.count >= SEM_LIMIT:
                if e.pending:
                    raise RuntimeError("sem rollover with pending ops")
                self._new_eng_sem(e)
            e.count += 1
            ins.then_inc(self.sems[e.sem], 1)
            val = e.count
            self.hist[(e.sem, val)] = dict(e.known)
            e.pending = False
        else:
            if e.count + 1 > SEM_LIMIT:
                raise RuntimeError("unsignaled op at sem rollover")
            val = e.count + 1
            e.pending = True
        self._record(e.sem, val, reads, writes)
        return ins

    def dma(self, q, out, in_, reads=(), writes=(), **kw):
        if self.dry:
            return None
        e = self.E[q]
        pool = self.dma_pool[q]
        i = pool["i"]
        pool["i"] = (i + 1) % len(pool["keys"])
        key = pool["keys"][i]
        need, own_raw = self._deps(e, reads, writes)
        if pool["vals"][i] > 0:
            need[key] = max(need.get(key, 0), pool["vals"][i])
        self._wait(e, need, own_raw)
        ins = e.h.dma_start(out=out, in_=in_, **kw)
        self.n_ins += 1
        pool["vals"][i] += 16
        val = pool["vals"][i]
        ins.then_inc(self.sems[key], 16)
        self.hist[(key, val)] = dict(e.known)
        self._record(key, val, reads, writes)
        return ins

    def finish(self, outs):
        if self.dry:
            return
        e = self.E["sp"]
        need = {}
        for r in outs:
            for k, v in r.w.items():
                if need.get(k, 0) < v:
                    need[k] = v
        for en in self.E.values():
            if en.pending:
                raise RuntimeError(f"engine {en.name} has pending unsignaled ops")
            if en.count > 0 and en is not e:
                need[en.sem] = max(need.get(en.sem, 0), en.count)
        for pool in self.dma_pool.values():
            for k, v in zip(pool["keys"], pool["vals"]):
                if v > 0:
                    need[k] = max(need.get(k, 0), v)
        self._wait(e, need, {})

    def close(self):
        self.stack.close()


class Arena:
    def __init__(self, nc, nbytes):
        self.t = nc.alloc_sbuf_tensor("arena", [128, nbytes], U8)
        self.nbytes = nbytes
        self.top = 0
        self.marks = []

    def alloc(self, shape, dt, parts=128):
        esz = 4 if dt == F32 else 2
        n = 1
        for s in shape:
            n *= s
        nb = (n * esz + 31) // 32 * 32
        off = self.top
        if off + nb > self.nbytes:
            raise RuntimeError(f"arena overflow: need {off + nb} > {self.nbytes}")
        self.top = off + nb
        v = self.t[0:parts, off:off + n * esz].bitcast(dt)
        if len(shape) == 2:
            v = v.rearrange("p (a b) -> p a b", b=shape[1])
        elif len(shape) == 3:
            v = v.rearrange("p (a b c) -> p a b c", b=shape[1], c=shape[2])
        return v

    def mark(self):
        return self.top

    def reset(self, m):
        self.top = m


def build_program():
    nc = bass.Bass("TRN2", target_bir_lowering=False)

    def din(name, shape):
        return nc.dram_tensor(name, list(shape), F32, kind="ExternalInput").ap()

    def dout(name, shape):
        return nc.dram_tensor(name, list(shape), F32, kind="ExternalOutput").ap()

    xm = din("xm", [1024, D]); xh = din("xh", [HALO, D]); xpre = din("xpre", [NPRE, D]); xs = din("xs", [NSAMP, D])
    pmd = din("pm", [1024, PLE]); psd = din("ps", [NSAMP, PLE])
    sgla = din("sgla", [NSAMP, NH, DK, DV]); spool = din("spool", [NSAMP, 15, 1024]); sconv = din("sconv", [NSAMP, 2, DFF])
    w_in = din("w_in", [D, INW]); wg_d = din("wg_aug", [17, 1024]); w_a = din("w_a", [D, D]); w_pool = din("w_pool", [4, 256, 256])
    w_b = din("w_b", [1024, D]); w_out = din("w_out", [D, D]); w_up = din("w_up", [D, 2 * DFF]); w_down = din("w_down", [DFF, D])
    w_pg = din("w_pg", [D, D]); w_ple = din("w_ple", [PLE, D]); gfin_d = din("gfin", [D])
    cpk_d = din("cpk", [128, C_TOT]); selw_d = din("selw", [120, 32])
    y_d = dout("y", [1024, D]); ys_d = dout("ys", [NSAMP, D])
    glap_d = dout("gla_p", [NH, DK, DV]); poolp_d = dout("pool_p", [15, 1024]); convp_d = dout("conv_p", [2, DFF])
    glas_d = dout("gla_s", [NSAMP, NH, DK, DV]); pools_d = dout("pool_s", [NSAMP, 15, 1024]); convs_d = dout("conv_s", [NSAMP, 2, DFF])

    fw = FW(nc)
    dbg_list = []

    def dbg(name, view, reads, p):
        if not DEBUG or p != DEBUG_PASS or fw.dry:
            return
        shp = list(view.shape)
        t = nc.dram_tensor("dbg_" + name, shp, view.dtype, kind="ExternalOutput").ap()
        r_ = Res("dbg_" + name)
        fw.dma("sp", t, view, reads=reads, writes=[r_])
        dbg_list.append(r_)
    fw.make_dma_pool("sp", 12)
    fw.make_dma_pool("pool", 4)
    outs_res = [Res("o_" + n) for n in ("y", "ys", "glap", "poolp", "convp", "glas", "pools", "convs")]
    R_y, R_ys, R_glap, R_poolp, R_convp, R_glas, R_pools, R_convs = outs_res

    ar = Arena(nc, (nc.sbuf_bytes_remaining - 64) // 256 * 256)
    Dps = [nc.alloc_psum_tensor(f"D{i}", [128, 1024], F32) for i in range(4)]
    bankR = [Res(f"bank{i}") for i in range(8)]

    def bank_ap(b):
        return Dps[b // 2][:, (b % 2) * 512:(b % 2) * 512 + 512]

    st = {"pp": 0, "cur": 0, "alt": 0}

    def take1():
        b = st["pp"]
        st["pp"] = (b + 1) % 6
        return bank_ap(b), [bankR[b]]

    def take2():
        b = st["pp"]
        if b % 2:
            b = (b + 1) % 6
        st["pp"] = (b + 2) % 6
        return Dps[b // 2], [bankR[b], bankR[b + 1]]

    def bf16v(ap512):
        return ap512.bitcast(BF16)

    def alt_eng():
        st["alt"] ^= 1
        return "act" if st["alt"] else "dve"

    def copy_op(eng, out, in_, reads, writes):
        if eng == "act":
            fw.op("act", lambda h: h.copy(out=out, in_=in_), reads=reads, writes=writes)
        else:
            fw.op(eng, lambda h: h.tensor_copy(out=out, in_=in_), reads=reads, writes=writes)

    ring = [ar.alloc([RING_ELEMS], BF16) for _ in range(RING_NB)]
    ringR = [Res(f"ring{i}") for i in range(RING_NB)]
    hT = ar.alloc([6, D], F32)
    hR = [Res(f"h{i}") for i in range(6)]
    aT = ar.alloc([KC, 552], BF16); aTR = Res("aT")
    S_ = ar.alloc([NH * 2, DV], F32); SR = [Res(f"S{i}") for i in range(NH * 2)]
    Sbf = ar.alloc([NH * 2, DV], BF16); SbfR = [Res(f"Sbf{i}") for i in range(NH * 2)]
    a_tm = ar.alloc([D], BF16); a_tmR = Res("a_tm")
    cpk = ar.alloc([C_TOT], F32); cpkR = Res("cpk")
    ident_bf = ar.alloc([128], BF16); identbR = Res("identb")
    wg = ar.alloc([1024], F32, parts=17); wgR = Res("wg")
    stats = ar.alloc([64], F32)
    statR = [Res(f"stat{i}") for i in range(64)]
    uhist = ar.alloc([8, 15], F32); uhistR = Res("uhist")
    graw10 = ar.alloc([FC, 10], F32); graw10R = Res("graw10")
    ghist = ar.alloc([FC, 2], F32); ghistR = Res("ghist")
    zmark = ar.mark()

    identf = cpk[:, C_ID:C_ID + 128]
    Ucum = cpk[:, C_UC:C_UC + 128]
    mask01 = cpk[:, C_MK:C_MK + 128]
    oh_row = cpk[:, C_OH:C_OH + 64].rearrange("p (s t) -> p s t", t=8)
    corr = cpk[:, C_CORR:C_CORR + 64].rearrange("p (g t) -> p g t", t=16)
    gT = cpk[:, C_GT:C_GT + 48].rearrange("p (i c) -> p i c", c=16)
    gnormT = cpk[:, C_GN:C_GN + 16]
    pscale = cpk[:, C_PS:C_PS + 8]
    convw = cpk[:, C_CW:C_CW + 172].rearrange("p (j f) -> p j f", f=FC)

    stc = {"i": 0}

    def stat_col():
        i = stc["i"]
        stc["i"] = (i + 1) % 64
        return stats[:, i:i + 1], statR[i]

    plan = []

    def issue_block(i):
        if i >= len(plan):
            return
        view, kc, ncols = plan[i]
        slot = i % RING_NB
        dst = ring[slot][:, 0:kc * ncols].rearrange("p (k n) -> p k n", n=ncols)
        fw.dma("pool", dst, view, writes=[ringR[slot]])

    def next_block(view, kc, ncols):
        i = st["cur"]
        st["cur"] = i + 1
        if fw.dry:
            plan.append((view, kc, ncols))
        else:
            issue_block(i + RING_NB - 1)
        slot = i % RING_NB
        return ring[slot][:, 0:kc * ncols].rearrange("p (k n) -> p k n", n=ncols), [ringR[slot]]

    def wview(w, r0, nk, c0, ncols):
        return w[r0:r0 + nk * 128, c0:c0 + ncols].rearrange("(k p) n -> p k n", p=128)

    def mm(out, lhsT, rhs, start, stop, reads, writes, signal):
        fw.op("pe", lambda h: h.matmul(out, lhsT=lhsT, rhs=rhs, start=start, stop=stop),
              reads=reads, writes=writes, signal=signal)

    def fm_group(Dt, DR, M, lhs_of_k, rhsT, nk, colsA, colsB, rd):
        a0, a1 = colsA
        for k in range(nk):
            mm(Dt[0:M, 0:a1 - a0], lhs_of_k(k), rhsT[:, k, a0:a1], k == 0, k == nk - 1, rd, DR,
               signal=(k == nk - 1) and colsB is None)
        if colsB is not None:
            b0, b1 = colsB
            for k in range(nk):
                mm(Dt[0:M, 512:512 + b1 - b0], lhs_of_k(k), rhsT[:, k, b0:b1], k == 0, k == nk - 1, rd, DR,
                   signal=(k == nk - 1))

    def tm_proj(w, c0w, subs, lhsT_of, lhsR, evac, pump_fn=None):
        banks = [take1() for _ in subs]
        for kh in range(2):
            blk, bkR = next_block(wview(w, kh * 8 * 128, 8, c0w, 512), 8, 512)
            for si, (slot, M, c0) in enumerate(subs):
                bk, bR = banks[si]
                for k in range(8):
                    mm(bk[0:M, 0:512], lhsT_of(kh * 8 + k, c0, M), blk[:, k, :], kh == 0 and k == 0, kh == 1 and k == 7,
                       bkR + lhsR, bR, signal=(k == 7))
                if pump_fn is not None:
                    pump_fn()
        for si, (slot, M, c0) in enumerate(subs):
            bk, bR = banks[si]
            evac(slot, M, c0, bk, bR)

    def tm_group(bank, bR, M, lhsT_of_k, rhs_of_k, nk, ncols, rd, first=True, last=True):
        for k in range(nk):
            mm(bank[0:M, 0:ncols], lhsT_of_k(k), rhs_of_k(k), first and k == 0, last and k == nk - 1, rd, bR,
               signal=(k == nk - 1))

    def rstd_from(ss, ssR, n):
        lnv, lnR = stat_col()
        rs, rsR = stat_col()
        return lnv, lnR, rs, rsR

    def norm_to_aT(subs, gi):
        for (slot, M, c0) in subs:
            ss, ssR = stat_col()
            lnv, lnR = stat_col()
            rs, rsR = stat_col()
            fw.op("act", lambda h: h.activation(out=a_tm[0:M, :], in_=hT[0:M, slot, :], func=AF.Square, accum_out=ss[0:M, :]),
                  reads=[hR[slot]], writes=[a_tmR, ssR])
            fw.op("act", lambda h: h.activation(out=lnv[0:M, :], in_=ss[0:M, :], func=AF.Ln, scale=1.0 / D, bias=EPS),
                  reads=[ssR], writes=[lnR])
            fw.op("act", lambda h: h.activation(out=rs[0:M, :], in_=lnv[0:M, :], func=AF.Exp, scale=-0.5),
                  reads=[lnR], writes=[rsR])
            fw.op("dve", lambda h: h.tensor_scalar(out=a_tm[0:M, :], in0=hT[0:M, slot, :], scalar1=rs[0:M, 0:1], scalar2=None, op0=ALU.mult),
                  reads=[hR[slot], rsR], writes=[a_tmR])
            Dt, DR = take2()
            pv = Dt[:, :].bitcast(BF16).rearrange("p (c m) -> p c m", m=128)
            for c in range(KC):
                fw.op("pe", lambda h: h.transpose(out=pv[:, c, 0:M], in_=a_tm[0:M, c * 128:(c + 1) * 128], identity=ident_bf[0:M, 0:M]),
                      reads=[a_tmR, identbR], writes=DR, signal=(c == KC - 1))
            gb = gT[:, gi, :].unsqueeze(2).to_broadcast([128, KC, M])
            fw.op("dve", lambda h: h.tensor_tensor(out=aT[:, :, c0:c0 + M], in0=pv[:, :, 0:M], in1=gb, op=ALU.mult),
                  reads=DR + [cpkR], writes=[aTR])

    def rstd_small(ss, ssR, M, n):
        lnv, lnR = stat_col()
        rs, rsR = stat_col()
        fw.op("act", lambda h: h.activation(out=lnv[0:M, :], in_=ss[0:M, :], func=AF.Ln, scale=1.0 / n, bias=EPS),
              reads=[ssR], writes=[lnR])
        fw.op("act", lambda h: h.activation(out=rs[0:M, :], in_=lnv[0:M, :], func=AF.Exp, scale=-0.5),
              reads=[lnR], writes=[rsR])
        return rs, rsR

    def emit():
        st["pp"] = 0; st["cur"] = 0; st["alt"] = 0; stc["i"] = 0
        ar.reset(zmark)
        if not fw.dry:
            for i in range(RING_NB - 1):
                issue_block(i)
        fw.dma("sp", cpk, cpk_d, writes=[cpkR])
        fw.dma("sp", wg, wg_d, writes=[wgR])
        fw.op("dve", lambda h: h.tensor_copy(out=ident_bf, in_=identf), reads=[cpkR], writes=[identbR])
        for hh in range(NH):
            fw.op("dve", lambda h: h.memset(S_[:, 2 * hh:2 * hh + 2, :], 0.0), writes=[SR[2 * hh], SR[2 * hh + 1]])
        fw.op("dve", lambda h: h.memset(uhist, 0.0), writes=[uhistR])
        fw.op("dve", lambda h: h.memset(ghist, 0.0), writes=[ghistR])

        def gla_decay(hh, tsubs, glr_aug, glrR, g1s, bfm, bfmR, Ah, AhR):
            for ti, (slot, M, c0) in enumerate(tsubs):
                g1, g1R = g1s[ti]
                bk, bR = take1()
                mm(bk[0:M, 0:256], glr_aug[:, c0:c0 + M], wg[:, hh * 256:(hh + 1) * 256], True, True, [glrR, wgR], bR, True)
                fw.op("act", lambda h: h.activation(out=g1[0:M, :], in_=bk[0:M, 0:256], func=AF.Exp, scale=-1.0),
                      reads=bR, writes=[g1R])
                fw.op("act", lambda h: h.activation(out=g1[0:M, :], in_=g1[0:M, :], func=AF.Ln, bias=1.0),
                      reads=[g1R], writes=[g1R])
            for ti, (slot, M, c0) in enumerate(tsubs):
                g1, g1R = g1s[ti]
                bk2, bR2 = take1()
                for c in range(2):
                    mm(bk2[:, c * 128:c * 128 + M], g1[0:M, c * 128:(c + 1) * 128], Ucum[0:M, 0:M], True, True,
                       [g1R, cpkR], bR2, c == 1)
                src = bk2[:, 0:256].rearrange("p (c m) -> p c m", m=128)[:, :, 0:M]
                copy_op("dve", bfm[:, :, c0:c0 + M], src, bR2, [bfmR])
                fw.op("act", lambda h: h.activation(out=Ah[:, :, ti:ti + 1], in_=bfm[:, :, c0 + M - 1:c0 + M], func=AF.Exp),
                      reads=[bfmR], writes=[AhR])

        def s_update(hh, ti, M, kd_tm, kdR, vsl, vR, Ah, AhR, cast):
            for c in range(2):
                bk, bR = take1()
                mm(bk[:, :], kd_tm[0:M, c * 128:(c + 1) * 128], vsl, True, True, [kdR, vR], bR, True)
                Sv = S_[:, 2 * hh + c, :]
                fw.op("dve", lambda h: h.scalar_tensor_tensor(out=Sv, in0=Sv, scalar=Ah[:, c, ti:ti + 1], in1=bk[:, :],
                                                                op0=ALU.mult, op1=ALU.add),
                      reads=bR + [AhR, SR[2 * hh + c]], writes=[SR[2 * hh + c]])
            if cast:
                fw.op("act", lambda h: h.copy(out=Sbf[:, 2 * hh:2 * hh + 2, :], in_=S_[:, 2 * hh:2 * hh + 2, :]),
                      reads=[SR[2 * hh], SR[2 * hh + 1]], writes=[SbfR[2 * hh], SbfR[2 * hh + 1]])

        def kd_transpose(M, c0, kdT, kdTR, kd_tm, kdR):
            bk, bR = take1()
            bv = bf16v(bk)
            for c in range(2):
                fw.op("pe", lambda h: h.transpose(out=bv[0:M, c * 128:(c + 1) * 128], in_=kdT[:, c, c0:c0 + M], identity=ident_bf),
                      reads=[kdTR, identbR], writes=bR, signal=(c == 1))
            copy_op("act", kd_tm[0:M, :], bv[0:M, 0:256], bR, [kdR])

        def prefix_pass(pi):
            fw.new_phase()
            ar.reset(zmark)
            sizes = PRE_SUBS[pi]
            subs = []
            c0 = 0
            for j, M in enumerate(sizes):
                subs.append((j, M, c0))
                c0 += M
            NP = c0
            tok0 = sum(sum(s) for s in PRE_SUBS[:pi])
            for (slot, M, cc) in subs:
                fw.dma("sp", hT[0:M, slot, :], xpre[tok0 + cc:tok0 + cc + M, :], writes=[hR[slot]])
            norm_to_aT(subs, 0)
            glr_aug = ar.alloc([552], F32, parts=17); glrR = fw.res("glr")
            g1s = [(ar.alloc([256], F32), fw.res(f"g1_{i}")) for i in range(4)]
            bfm = ar.alloc([2, 544], F32); bfmR = fw.res("bfm")
            Ah = ar.alloc([2, 8], F32); AhR = fw.res("Ah")
            Etmp = ar.alloc([544], F32); EtR = fw.res("Et")
            kdT = ar.alloc([2, 552], BF16); kdTR = fw.res("kdT")
            vh = ar.alloc([6, DV], BF16); vR = fw.res("vh")
            kd_tm = ar.alloc([256], BF16); kdR = fw.res("kdtm")
            fw.op("dve", lambda h: h.memset(glr_aug, 1.0), writes=[glrR])
            blk, bkR = next_block(wview(w_in, 0, KC, OGLR, 16), KC, 16)
            Dt, DR = take2()
            fm_group(Dt, DR, 16, lambda k: blk[:, k, 0:16], aT, KC, (0, NP), None, bkR + [aTR])
            copy_op("act", glr_aug[0:16, 0:NP], Dt[0:16, 0:NP], DR, [glrR])
            for hh in range(NH):
                gla_decay(hh, subs, glr_aug, glrR, g1s, bfm, bfmR, Ah, AhR)
                blk, bkR = next_block(wview(w_in, 0, KC, OK_ + hh * 256, 256), KC, 256)
                for c in range(2):
                    Dt, DR = take2()
                    fm_group(Dt, DR, 128, lambda k: blk[:, k, c * 128:(c + 1) * 128], aT, KC, (0, NP), None, bkR + [aTR])
                    for (slot, M, cc) in subs:
                        fw.op("act", lambda h: h.activation(out=Etmp[:, cc:cc + M], in_=bfm[:, c, cc:cc + M], func=AF.Exp, scale=-1.0,
                                                              bias=bfm[:, c, cc + M - 1:cc + M]),
                              reads=[bfmR], writes=[EtR])
                    fw.op("dve", lambda h: h.tensor_tensor(out=kdT[:, c, 0:NP], in0=Dt[:, 0:NP], in1=Etmp[:, 0:NP], op=ALU.mult),
                          reads=DR + [EtR], writes=[kdTR])
                tm_proj(w_in, OV + hh * 512, subs, lambda k, c0, M: aT[:, k, c0:c0 + M], [aTR],
                        lambda slot, M, c0, bk, bR: copy_op(alt_eng(), vh[0:M, slot, :], bk[0:M, :], bR, [vR]))
                for ti, (slot, M, cc) in enumerate(subs):
                    kd_transpose(M, cc, kdT, kdTR, kd_tm, kdR)
                    s_update(hh, ti, M, kd_tm, kdR, vh[0:M, slot, :], vR, Ah, AhR, cast=(pi == 1 and ti == len(subs) - 1))

        def main_pass(p):
            fw.new_phase()
            ar.reset(zmark)
            if p == 0:
                halo_c0, samp_c0, NC = 512, 544, 552
            else:
                halo_c0, samp_c0, NC = None, 512, 520
            NP = samp_c0
            colsA, colsB = (0, 512), (512, NC)
            mains = [(j, 128, j * 128) for j in range(4)]
            samp = (4, SPP, samp_c0)
            halo = (5, HALO, halo_c0) if p == 0 else None
            allsubs = mains + [samp] + ([halo] if halo else [])
            subs5 = mains + [samp]
            tsubs = ([halo] if halo else []) + mains
            s0 = p * SPP
            for (slot, M, c0) in mains:
                fw.dma("sp", hT[:, slot, :], xm[p * 512 + slot * 128:p * 512 + slot * 128 + 128, :], writes=[hR[slot]])
            fw.dma("sp", hT[0:SPP, 4, :], xs[s0:s0 + SPP, :], writes=[hR[4]])
            if halo:
                fw.dma("sp", hT[0:HALO, 5, :], xh, writes=[hR[5]])
            norm_to_aT(allsubs, 0)
            dbg("aT", aT, [aTR], p)

            ogT = ar.alloc([KC, 552], BF16); ogTR = fw.res("ogT")
            z1a = ar.mark()
            glr_aug = ar.alloc([552], F32, parts=17); glrR = fw.res("glr")
            g1 = ar.alloc([256], F32); g1R = fw.res("g1")
            bfm = ar.alloc([2, 544], F32); bfmR = fw.res("bfm")
            Ah = ar.alloc([2, 8], F32); AhR = fw.res("Ah")
            Et = [ar.alloc([544], F32) for _ in range(2)]; EtR = [fw.res(f"Et{i}") for i in range(2)]
            g1s = [(g1, g1R), (Et[0][:, 0:256], EtR[0]), (Et[0][:, 272:528], EtR[0]), (Et[1][:, 0:256], EtR[1]), (Et[1][:, 272:528], EtR[1])]
            qbT = ar.alloc([2, 552], BF16); qbTR = fw.res("qbT")
            kbT = ar.alloc([2, 552], BF16); kbTR = fw.res("kbT")
            kdT = ar.alloc([2, 552], BF16); kdTR = fw.res("kdT")
            vh = ar.alloc([5, DV], BF16); vR = fw.res("vh")
            gr = ar.alloc([5, DV], BF16); grR = [fw.res(f"gr{i}") for i in range(5)]
            vs = [ar.alloc([DV], BF16, parts=8) for _ in range(2)]; vsR = [fw.res(f"vs{i}") for i in range(2)]
            grs = [ar.alloc([DV], BF16, parts=8) for _ in range(2)]; grsR = [fw.res(f"grs{i}") for i in range(2)]
            ks_sb = [ar.alloc([256], F32, parts=8) for _ in range(2)]; ksbR = [fw.res(f"ksb{i}") for i in range(2)]
            att_sb = [ar.alloc([128], BF16) for _ in range(2)]; attR = [fw.res(f"att{i}") for i in range(2)]
            kd_tm = [ar.alloc([256], BF16) for _ in range(2)]; kdR = [fw.res(f"kdtm{i}") for i in range(2)]
            qsT = ar.alloc([8, 8], F32); qsTR = fw.res("qsT")
            ksT = ar.alloc([8, 8], F32); ksTR = fw.res("ksT")
            aTs = ar.alloc([8, 8], F32); aTsR = fw.res("aTs")
            etmp_s = ar.alloc([8, 8], F32); etsR = fw.res("etmps")
            kmask = ar.alloc([2, 256], BF16, parts=8); kmR = [fw.res(f"kmask{i}") for i in range(2)]
            qmask = [ar.alloc([2, 8, 8], BF16) for _ in range(2)]; qmR = [fw.res(f"qmask{i}") for i in range(2)]
            Sin = [ar.alloc([DV], F32) for _ in range(3)]; SinR = [fw.res(f"Sin{i}") for i in range(3)]
            Snew = [ar.alloc([DV], F32) for _ in range(3)]; SnewR = [fw.res(f"Snew{i}") for i in range(3)]
            Snbf = [ar.alloc([DV], BF16) for _ in range(3)]; SnbfR = [fw.res(f"Snbf{i}") for i in range(3)]

            fw.op("dve", lambda h: h.memset(glr_aug, 1.0), writes=[glrR])
            blk, bkR = next_block(wview(w_in, 0, KC, OGLR, 16), KC, 16)
            Dt, DR = take2()
            fm_group(Dt, DR, 16, lambda k: blk[:, k, 0:16], aT, KC, colsA, colsB, bkR + [aTR])
            copy_op("act", glr_aug[0:16, 0:NC], Dt[0:16, 0:NC], DR, [glrR])
            bk, bR = take1()
            for c in range(8):
                mm(bk[:, c * 8:(c + 1) * 8], wg[:, c * 128:(c + 1) * 128], glr_aug[:, samp_c0:samp_c0 + SPP], True, True,
                   [wgR, glrR], bR, c == 7)
            zsv = bk[:, 0:64].rearrange("p (c s) -> p c s", s=8)
            fw.op("act", lambda h: h.activation(out=etmp_s, in_=zsv, func=AF.Exp, scale=-1.0), reads=bR, writes=[etsR])
            fw.op("act", lambda h: h.activation(out=etmp_s, in_=etmp_s, func=AF.Ln, bias=1.0), reads=[etsR], writes=[etsR])
            fw.op("act", lambda h: h.activation(out=aTs, in_=etmp_s, func=AF.Exp, scale=-1.0 / 16.0), reads=[etsR], writes=[aTsR])

            def vsl(slot):
                return 4 if slot == 5 else slot

            def o_epilogue(grv, grvR, M, bo, boR):
                ss, ssR = stat_col()
                fw.op("act", lambda h: h.activation(out=a_tm[0:M, 0:DV], in_=bo[0:M, :], func=AF.Square, accum_out=ss[0:M, :]),
                      reads=boR, writes=[a_tmR, ssR])
                rs, rsR = rstd_small(ss, ssR, M, DV)
                fw.op("dve", lambda h: h.scalar_tensor_tensor(out=grv, in0=bo[0:M, :], scalar=rs[0:M, 0:1], in1=grv,
                                                                op0=ALU.mult, op1=ALU.mult),
                      reads=boR + [rsR, grvR], writes=[grvR])

            def og_transpose(grv, grvR, M, c0, hx):
                bk, bR = take1()
                bv = bf16v(bk)
                for c4 in range(4):
                    fw.op("pe", lambda h: h.transpose(out=bv[:, c4 * 128:c4 * 128 + M], in_=grv[:, c4 * 128:(c4 + 1) * 128],
                                                      identity=ident_bf[0:M, 0:M]),
                          reads=[grvR, identbR], writes=bR, signal=(c4 == 3))
                src = bv[:, 0:512].rearrange("p (c m) -> p c m", m=128)[:, :, 0:M]
                gb = gnormT[:, 4 * hx:4 * hx + 4].unsqueeze(2).to_broadcast([128, 4, M])
                fw.op("dve", lambda h: h.tensor_tensor(out=ogT[:, 4 * hx:4 * hx + 4, c0:c0 + M], in0=src, in1=gb, op=ALU.mult),
                      reads=bR + [cpkR], writes=[ogTR])

            cur = {"gen": None, "need_safe": False}

            def pump(k, safe=False):
                for _ in range(k):
                    g_ = cur["gen"]
                    if g_ is None:
                        return
                    if cur["need_safe"] and not safe:
                        return
                    try:
                        tag = next(g_)
                    except StopIteration:
                        cur["gen"] = None
                        cur["need_safe"] = False
                        return
                    cur["need_safe"] = (tag == "final")

            def drain():
                pump(10 ** 6, safe=True)

            def sample_gen(hs, ps):
                bk7, bR7 = bank_ap(7), [bankR[7]]
                for c in range(2):
                    fw.op("pe", lambda h: h.transpose(out=bk7[0:SPP, c * 128:(c + 1) * 128], in_=ksT[:, 2 * hs + c, :], identity=identf),
                          reads=[ksTR, cpkR], writes=bR7, signal=(c == 1))
                copy_op("dve", ks_sb[ps], bk7[0:SPP, 0:256], bR7, [ksbR[ps]])
                for s in range(SPP):
                    fw.op("dve", lambda h: h.tensor_tensor(out=qmask[ps][:, :, s, :], in0=qsT[:, 2 * hs:2 * hs + 2, :],
                                                             in1=oh_row[:, s:s + 1, :].to_broadcast([128, 2, 8]), op=ALU.mult),
                          reads=[qsTR, cpkR], writes=[qmR[ps]])
                bos, bosR = bank_ap(6), [bankR[6]]
                LAG = 2
                items = [(s_, c_) for s_ in range(SPP) for c_ in range(2)]
                NI = len(items)

                def s_load(n_):
                    if n_ < NI:
                        sl_, cl_ = items[n_]
                        fw.dma("sp", Sin[n_ % 3], sgla[s0 + sl_, hs, cl_ * 128:(cl_ + 1) * 128, :], writes=[SinR[n_ % 3]])

                def mk_kmask(sk):
                    fw.op("dve", lambda h: h.tensor_scalar(out=kmask[:, sk % 2, :], in0=ks_sb[ps], scalar1=identf[0:SPP, sk:sk + 1], scalar2=None,
                                                             op0=ALU.mult),
                          reads=[ksbR[ps], cpkR], writes=[kmR[sk % 2]])
                mk_kmask(0)
                s_load(0)
                s_load(1)
                yield "item"
                for n in range(NI + LAG):
                    if n < NI:
                        s_, c_ = items[n]
                        k3 = n % 3
                        km = s_ % 2
                        s_load(n + 2)
                        if c_ == 0 and s_ + 1 < SPP:
                            mk_kmask(s_ + 1)
                        mm(bk7[:, :], kmask[:, km, c_ * 128:(c_ + 1) * 128], vs[ps], True, True, [kmR[km], vsR[ps]], bR7, True)
                        fw.op("dve", lambda h: h.scalar_tensor_tensor(out=Snew[k3], in0=Sin[k3], scalar=aTs[:, 2 * hs + c_, s_:s_ + 1], in1=bk7[:, :],
                                                                        op0=ALU.mult, op1=ALU.add),
                              reads=bR7 + [SinR[k3], aTsR], writes=[SnewR[k3]])
                        fw.dma("sp", glas_d[s0 + s_, hs, c_ * 128:(c_ + 1) * 128, :], Snew[k3], reads=[SnewR[k3]], writes=[R_glas])
                        fw.op("act", lambda h: h.copy(out=Snbf[k3], in_=Snew[k3]), reads=[SnewR[k3]], writes=[SnbfR[k3]])
                    m_ = n - LAG
                    if m_ >= 0:
                        s_, c_ = items[m_]
                        mm(bos[0:SPP, :], qmask[ps][:, c_, s_, :], Snbf[m_ % 3], m_ == 0, m_ == NI - 1, [qmR[ps], SnbfR[m_ % 3]], bosR, True)
                    yield "item"
                yield "final"
                o_epilogue(grs[ps], grsR[ps], SPP, bos, bosR)
                og_transpose(grs[ps], grsR[ps], SPP, samp_c0, hs)

            pump2 = {"n": 0}

            def pump_tm():
                pump2["n"] += 1
                if pump2["n"] % 2 == 0:
                    pump(1)

            for hh in range(NH):
                par = hh % 2
                def ev_r(slot, M, c0, bk, bR):
                    if slot == 4:
                        fw.op("act", lambda h: h.activation(out=grs[par], in_=bk[0:M, :], func=AF.Silu), reads=bR, writes=[grsR[par]])
                    else:
                        fw.op("act", lambda h: h.activation(out=gr[0:M, vsl(slot), :], in_=bk[0:M, :], func=AF.Silu),
                              reads=bR, writes=[grR[vsl(slot)]])
                tm_proj(w_in, OR + hh * 512, allsubs, lambda k, c0, M: aT[:, k, c0:c0 + M], [aTR], ev_r, pump_fn=pump_tm)
                gla_decay(hh, tsubs, glr_aug, glrR, g1s, bfm, bfmR, Ah, AhR)
                pump(1, safe=True)
                blk, bkR = next_block(wview(w_in, 0, KC, OQ + hh * 256, 256), KC, 256)
                for c in range(2):
                    Dt, DR = take2()
                    fm_group(Dt, DR, 128, lambda k: blk[:, k, c * 128:(c + 1) * 128], aT, KC, colsA, colsB, bkR + [aTR])
                    fw.op("act", lambda h: h.activation(out=Et[0][:, 0:NP], in_=bfm[:, c, 0:NP], func=AF.Exp),
                          reads=[bfmR], writes=[EtR[0]])
                    fw.op("dve", lambda h: h.scalar_tensor_tensor(out=qbT[:, c, 0:NP], in0=Dt[:, 0:NP], scalar=DK ** -0.5, in1=Et[0][:, 0:NP],
                                                                    op0=ALU.mult, op1=ALU.mult),
                          reads=DR + [EtR[0]], writes=[qbTR])
                    fw.op("dve", lambda h: h.tensor_scalar(out=qsT[:, 2 * hh + c, :], in0=Dt[:, samp_c0:samp_c0 + SPP], scalar1=DK ** -0.5,
                                                             scalar2=None, op0=ALU.mult),
                          reads=DR, writes=[qsTR])
                    pump(1, safe=True)
                blk, bkR = next_block(wview(w_in, 0, KC, OK_ + hh * 256, 256), KC, 256)
                for c in range(2):
                    Dt, DR = take2()
                    fm_group(Dt, DR, 128, lambda k: blk[:, k, c * 128:(c + 1) * 128], aT, KC, colsA, colsB, bkR + [aTR])
                    fw.op("act", lambda h: h.activation(out=Et[1][:, 0:NP], in_=bfm[:, c, 0:NP], func=AF.Exp, scale=-1.0),
                          reads=[bfmR], writes=[EtR[1]])
                    fw.op("dve", lambda h: h.tensor_tensor(out=kbT[:, c, 0:NP], in0=Dt[:, 0:NP], in1=Et[1][:, 0:NP], op=ALU.mult),
                          reads=DR + [EtR[1]], writes=[kbTR])
                    for (slot, M, cc) in tsubs:
                        fw.op("act", lambda h: h.activation(out=Et[0][:, cc:cc + M], in_=bfm[:, c, cc:cc + M], func=AF.Exp, scale=-1.0,
                                                              bias=bfm[:, c, cc + M - 1:cc + M]),
                              reads=[bfmR], writes=[EtR[0]])
                    fw.op("dve", lambda h: h.tensor_tensor(out=kdT[:, c, 0:NP], in0=Dt[:, 0:NP], in1=Et[0][:, 0:NP], op=ALU.mult),
                          reads=DR + [EtR[0]], writes=[kdTR])
                    copy_op("dve", ksT[:, 2 * hh + c, :], Dt[:, samp_c0:samp_c0 + SPP], DR, [ksTR])
                    pump(1, safe=True)
                def ev_v(slot, M, c0, bk, bR):
                    if slot == 4:
                        copy_op(alt_eng(), vs[par], bk[0:M, :], bR, [vsR[par]])
                    else:
                        copy_op(alt_eng(), vh[0:M, vsl(slot), :], bk[0:M, :], bR, [vR])
                tm_proj(w_in, OV + hh * 512, allsubs, lambda k, c0, M: aT[:, k, c0:c0 + M], [aTR], ev_v, pump_fn=pump_tm)

                def stageA(ti):
                    slot, M, c0 = tsubs[ti]
                    b2 = ti % 2
                    bk, bR = take1()
                    for c in range(2):
                        mm(bk[0:M, 0:M], kbT[:, c, c0:c0 + M], qbT[:, c, c0:c0 + M], c == 0, c == 1, [kbTR, qbTR], bR, c == 1)
                    fw.op("dve", lambda h: h.tensor_tensor(out=att_sb[b2][0:M, 0:M], in0=bk[0:M, 0:M], in1=mask01[0:M, 0:M], op=ALU.mult),
                          reads=bR + [cpkR], writes=[attR[b2]])
                    kd_transpose(M, c0, kdT, kdTR, kd_tm[b2], kdR[b2])

                NT_ = len(tsubs)
                stageA(0)
                for ti, (slot, M, c0) in enumerate(tsubs):
                    b2 = ti % 2
                    vi = vsl(slot)
                    if ti + 1 < NT_:
                        stageA(ti + 1)
                    kvb = [take1() for _ in range(2)]
                    for c in range(2):
                        mm(kvb[c][0][:, :], kd_tm[b2][0:M, c * 128:(c + 1) * 128], vh[0:M, vi, :], True, True, [kdR[b2], vR], kvb[c][1], True)
                    bo, boR = take1()
                    mm(bo[0:M, :], att_sb[b2][0:M, 0:M], vh[0:M, vi, :], True, False, [attR[b2], vR], boR, False)
                    for c in range(2):
                        mm(bo[0:M, :], qbT[:, c, c0:c0 + M], Sbf[:, 2 * hh + c, :], False, c == 1, [qbTR, SbfR[2 * hh + c]], boR, c == 1)
                    for c in range(2):
                        Sv = S_[:, 2 * hh + c, :]
                        fw.op("dve", lambda h: h.scalar_tensor_tensor(out=Sv, in0=Sv, scalar=Ah[:, c, ti:ti + 1], in1=kvb[c][0][:, :],
                                                                        op0=ALU.mult, op1=ALU.add),
                              reads=kvb[c][1] + [AhR, SR[2 * hh + c]], writes=[SR[2 * hh + c]])
                        fw.op("act", lambda h: h.copy(out=Sbf[:, 2 * hh + c, :], in_=Sv),
                              reads=[SR[2 * hh + c]], writes=[SbfR[2 * hh + c]])
                    o_epilogue(gr[0:M, vi, :], grR[vi], M, bo, boR)
                    if ti >= 1:
                        sl_, Mp_, cp_ = tsubs[ti - 1]
                        og_transpose(gr[0:Mp_, vsl(sl_), :], grR[vsl(sl_)], Mp_, cp_, hh)
                    pump(1, safe=True)
                sl_, Mp_, cp_ = tsubs[NT_ - 1]
                og_transpose(gr[0:Mp_, vsl(sl_), :], grR[vsl(sl_)], Mp_, cp_, hh)
                drain()
                cur["gen"] = sample_gen(hh, par)
            drain()
            if p == 1:
                fw.dma("sp", glap_d.rearrange("h (c p) v -> p (h c) v", p=128), S_, reads=SR, writes=[R_glap])

            dbg("ogT", ogT, [ogTR], p)
            fw.new_phase()
            ar.reset(z1a)
            pmwT = ar.alloc([8, 552], BF16); pmwTR = fw.res("pmwT")
            z1 = ar.mark()
            XL = 15 + 512
            X = [ar.alloc([2, XL], F32) for _ in range(3)]; XR = [fw.res(f"X{i}") for i in range(3)]
            XH = [ar.alloc([2, 15 + HALO], F32) for _ in range(3)]; XHR = [fw.res(f"XH{i}") for i in range(3)]
            pmT = ar.alloc([2, 552], BF16); pmTR = fw.res("pmT")
            us = ar.alloc([8, 8], F32); usR = fw.res("us")
            tsm = ar.alloc([8], F32); tsmR = fw.res("tsm")
            sp_tm = ar.alloc([1024], F32, parts=120); sptR = fw.res("sp_tm")
            selw = ar.alloc([32], F32, parts=120); selR = fw.res("selw")
            urow = ar.alloc([1024], F32, parts=23); urowR = fw.res("urow")
            fw.dma("sp", sp_tm, spool[s0:s0 + SPP].rearrange("s r c -> (s r) c"), writes=[sptR])
            fw.dma("sp", selw, selw_d, writes=[selR])
            if p == 0:
                fw.op("dve", lambda h: h.memset(XH[0], 0.0), writes=[XHR[0]])

            def windows(Xb, XbR, L, g):
                src = 0
                sh = 1
                for stp in range(g + 1):
                    dst = 1 if src != 1 else 2
                    eng = "dve"
                    lo = 2 * sh - 1
                    fw.op(eng, lambda h: h.tensor_tensor(out=Xb[dst][:, :, lo:L], in0=Xb[src][:, :, lo:L], in1=Xb[src][:, :, lo - sh:L - sh], op=ALU.add),
                          reads=[XbR[src]], writes=[XbR[dst]])
                    src = dst
                    sh *= 2
                return src

            uD = {}

            def u_mm(g):
                blk, bkR = next_block(wview(w_in, 0, KC, OU + g * 256, 256), KC, 256)
                for c in range(2):
                    Dt, DR = Dps[c], [bankR[2 * c], bankR[2 * c + 1]]
                    fm_group(Dt, DR, 128, lambda k: blk[:, k, c * 128:(c + 1) * 128], aT, KC, colsA, colsB, bkR + [aTR])
                    uD[(g, c)] = (Dt, DR)

            def u_evac(g):
                for c in range(2):
                    Dt, DR = uD[(g, c)]
                    copy_op("act", X[0][:, c, 15:XL], Dt[:, 0:512], DR, [XR[0]])
                    copy_op("dve", us[:, 2 * g + c, :], Dt[:, samp_c0:samp_c0 + SPP], DR, [usR])
                    if p == 0:
                        copy_op("dve", XH[0][:, c, 15:15 + HALO], Dt[:, 512:512 + HALO], DR, [XHR[0]])
                        copy_op("dve", X[0][:, c, 0:15], Dt[:, 512 + HALO - 15:512 + HALO], DR, [XR[0]])
                    else:
                        copy_op("dve", X[0][:, c, 0:15], uhist[:, 2 * g + c, :], [uhistR], [XR[0]])
                    copy_op("dve", uhist[:, 2 * g + c, :], X[0][:, c, XL - 15:XL], [XR[0]], [uhistR])

            u_mm(0)
            u_evac(0)
            for g in range(4):
                wd = 2 ** (g + 1)
                if g + 1 < 4:
                    u_mm(g + 1)
                f = windows(X, XR, XL, g)
                if p == 0:
                    fw.op("dve", lambda h: h.tensor_tensor(out=X[f][:, :, 15:31], in0=X[f][:, :, 15:31],
                                                             in1=corr[:, g:g + 1, :].to_broadcast([128, 2, 16]), op=ALU.mult),
                          reads=[XR[f], cpkR], writes=[XR[f]])
                fw.op("dve", lambda h: h.scalar_tensor_tensor(out=pmT[:, :, 0:512], in0=X[f][:, :, 15:XL], scalar=1.0 / wd, in1=X[0][:, :, 15:XL],
                                                                op0=ALU.mult, op1=ALU.subtract),
                      reads=[XR[f], XR[0]], writes=[pmTR])
                if p == 0:
                    fh = windows(XH, XHR, 15 + HALO, g)
                    fw.op("dve", lambda h: h.scalar_tensor_tensor(out=pmT[:, :, 512:512 + HALO], in0=XH[fh][:, :, 15:15 + HALO], scalar=1.0 / wd,
                                                                    in1=XH[0][:, :, 15:15 + HALO], op0=ALU.mult, op1=ALU.subtract),
                          reads=[XHR[fh], XHR[0]], writes=[pmTR])
                for c in range(2):
                    cc = 2 * g + c
                    bk, bR = bank_ap(6 + c), [bankR[6 + c]]
                    mm(bk[:, 0:SPP], sp_tm[:, cc * 128:(cc + 1) * 128], selw[:, g * 8:(g + 1) * 8], True, True, [sptR, selR], bR, True)
                    fw.op("dve", lambda h: h.tensor_scalar(out=tsm, in0=us[:, cc, :], scalar1=1.0 / wd - 1.0, scalar2=None, op0=ALU.mult),
                          reads=[usR], writes=[tsmR])
                    fw.op("dve", lambda h: h.scalar_tensor_tensor(out=pmT[:, c, samp_c0:samp_c0 + SPP], in0=bk[:, 0:SPP], scalar=1.0 / wd, in1=tsm,
                                                                    op0=ALU.mult, op1=ALU.add),
                          reads=bR + [tsmR], writes=[pmTR])
                blk, bkR = next_block(w_pool[g].rearrange("(k p) n -> p k n", p=128), 2, 256)
                for dc in range(2):
                    Dt, DR = Dps[2], [bankR[4], bankR[5]]
                    fm_group(Dt, DR, 128, lambda k: blk[:, k, dc * 128:(dc + 1) * 128], pmT, 2, colsA, colsB, bkR + [pmTR])
                    fw.op("act", lambda h: h.activation(out=pmwT[:, 2 * g + dc, 0:NC], in_=Dt[:, 0:NC], func=AF.Copy, scale=pscale[:, 2 * g + dc:2 * g + dc + 1]),
                          reads=DR + [cpkR], writes=[pmwTR])
                if g + 1 < 4:
                    u_evac(g + 1)
            st["pp"] = 0
            Dt, DR = take2()
            for cc in range(8):
                fw.op("pe", lambda h: h.transpose(out=Dt[0:SPP, cc * 128:(cc + 1) * 128], in_=us[:, cc, :], identity=identf),
                      reads=[usR, cpkR], writes=DR, signal=(cc == 7))
            copy_op("act", urow[0:SPP, :], Dt[0:SPP, :], DR, [urowR])
            fw.dma("sp", pools_d[s0:s0 + SPP, 14, :], urow[0:SPP, :], reads=[urowR], writes=[R_pools])
            fw.dma("sp", pools_d[s0:s0 + SPP, 0:14, :], spool[s0:s0 + SPP, 1:15, :], writes=[R_pools])
            if p == 1:
                Dt, DR = take2()
                for cc in range(8):
                    fw.op("pe", lambda h: h.transpose(out=Dt[0:15, cc * 128:(cc + 1) * 128], in_=uhist[:, cc, :], identity=identf),
                          reads=[uhistR, cpkR], writes=DR, signal=(cc == 7))
                copy_op("act", urow[0:15, :], Dt[0:15, :], DR, [urowR])
                fw.dma("sp", poolp_d, urow[0:15, :], reads=[urowR], writes=[R_poolp])

            dbg("pmwT", pmwT, [pmwTR], p)
            fw.new_phase()
            ar.reset(z1)
            mT = ar.alloc([KC, 552], BF16); mTR = fw.res("mT")
            sga = ar.alloc([4, 552], F32); sgaR = [fw.res(f"sga{i}") for i in range(4)]
            m1 = ar.alloc([4, 552], F32); m1R = [fw.res(f"m1{i}") for i in range(4)]
            for hb in range(8):
                blk, bkR = next_block(wview(w_in, 0, KC, OGA + hb * 256, 256), KC, 256)
                for n in range(2):
                    Dt, DR = take2()
                    fm_group(Dt, DR, 128, lambda k: blk[:, k, n * 128:(n + 1) * 128], aT, KC, colsA, colsB, bkR + [aTR])
                    fw.op("act", lambda h: h.activation(out=sga[:, n, 0:NC], in_=Dt[:, 0:NC], func=AF.Sigmoid), reads=DR, writes=[sgaR[n]])
                blk, bkR = next_block(wview(w_a, 0, KC, hb * 256, 256), KC, 256)
                for n in range(2):
                    Dt, DR = take2()
                    fm_group(Dt, DR, 128, lambda k: blk[:, k, n * 128:(n + 1) * 128], ogT, KC, colsA, colsB, bkR + [ogTR])
                    fw.op("dve", lambda h: h.tensor_tensor(out=m1[:, n, 0:NC], in0=Dt[:, 0:NC], in1=sga[:, n, 0:NC], op=ALU.mult),
                          reads=DR + [sgaR[n]], writes=[m1R[n]])
                blk, bkR = next_block(wview(w_in, 0, KC, OGB + hb * 256, 256), KC, 256)
                for n in range(2):
                    Dt, DR = take2()
                    fm_group(Dt, DR, 128, lambda k: blk[:, k, n * 128:(n + 1) * 128], aT, KC, colsA, colsB, bkR + [aTR])
                    fw.op("act", lambda h: h.activation(out=sga[:, 2 + n, 0:NC], in_=Dt[:, 0:NC], func=AF.Sigmoid), reads=DR, writes=[sgaR[2 + n]])
                blk, bkR = next_block(wview(w_b, 0, 8, hb * 256, 256), 8, 256)
                for n in range(2):
                    Dt, DR = take2()
                    fm_group(Dt, DR, 128, lambda k: blk[:, k, n * 128:(n + 1) * 128], pmwT, 8, colsA, colsB, bkR + [pmwTR])
                    fw.op("dve", lambda h: h.tensor_tensor(out=sga[:, 2 + n, 0:NC], in0=Dt[:, 0:NC], in1=sga[:, 2 + n, 0:NC], op=ALU.mult),
                          reads=DR + [sgaR[2 + n]], writes=[sgaR[2 + n]])
                    fw.op("dve", lambda h: h.tensor_tensor(out=mT[:, hb * 2 + n, 0:NC], in0=sga[:, 2 + n, 0:NC], in1=m1[:, n, 0:NC], op=ALU.add),
                          reads=[sgaR[2 + n], m1R[n]], writes=[mTR])
            dbg("mT", mT, [mTR], p)
            for nb in range(4):
                def ev_out(slot, M, c0, bk, bR):
                    hv = hT[0:M, slot, nb * 512:(nb + 1) * 512]
                    fw.op("dve", lambda h: h.tensor_tensor(out=hv, in0=hv, in1=bk[0:M, :], op=ALU.add), reads=bR + [hR[slot]], writes=[hR[slot]])
                tm_proj(w_out, nb * 512, allsubs, lambda k, c0, M: mT[:, k, c0:c0 + M], [mTR], ev_out)

            dbg("h1", hT, hR, p)
            norm_to_aT(allsubs, 1)
            dbg("cT", aT, [aTR], p)
            fw.new_phase()
            ar.reset(zmark)
            actT = ar.alloc([FC, 520], BF16); actTR = fw.res("actT")
            gext = [ar.alloc([514], F32) for _ in range(2)]; gextR = [fw.res(f"gext{i}") for i in range(2)]
            yt = [ar.alloc([512], F32) for _ in range(4)]; ytR = [fw.res(f"yt{i}") for i in range(4)]
            us_raw = ar.alloc([FC, 8], F32); usrR = fw.res("us_raw")
            stT = ar.alloc([FC, 16], F32); stTR = fw.res("stT")
            sct = [ar.alloc([512], F32, parts=16) for _ in range(2)]; sctR = [fw.res(f"sct{i}") for i in range(2)]
            grow = [ar.alloc([512], F32, parts=10) for _ in range(2)]; growR = [fw.res(f"grow{i}") for i in range(2)]
            tA = ar.alloc([FC, 8], F32); tAR = fw.res("tA")
            tB = ar.alloc([FC, 8], F32); tBR = fw.res("tB")
            if p == 0:
                gB = (542, 552)
            else:
                gB = (512, 520)
            uB = (samp_c0, samp_c0 + SPP)
            sconv_v = sconv[s0:s0 + SPP].rearrange("s r f -> (s r) f")
            NBLK = 11

            def sconv_load(j):
                ncb_ = 4 if j < 10 else 3
                fw.dma("sp", sct[j % 2][:, 0:ncb_ * 128], sconv_v[:, j * 512:j * 512 + ncb_ * 128], writes=[sctR[j % 2]])

            def sconv_tr(j):
                ncb_ = 4 if j < 10 else 3
                k2_ = j % 2
                bk, bR = bank_ap(7), [bankR[7]]
                for n_ in range(ncb_):
                    fw.op("pe", lambda h: h.transpose(out=bk[:, n_ * 16:(n_ + 1) * 16], in_=sct[k2_][:, n_ * 128:(n_ + 1) * 128], identity=identf[0:16, 0:16]),
                          reads=[sctR[k2_], cpkR], writes=bR, signal=(n_ == ncb_ - 1))
                copy_op("act", stT[:, 4 * j:4 * j + ncb_, :], bk[:, 0:ncb_ * 16].rearrange("p (n s) -> p n s", s=16), bR, [stTR])
            for i in range(22):
                ncb = 2 if i < 21 else 1
                if i % 2 == 0 and i // 2 < NBLK:
                    sconv_load(i // 2)
                if i % 2 == 1 and i // 2 < NBLK:
                    sconv_tr(i // 2)
                blkg, bgR = next_block(wview(w_up, 0, KC, i * 256, ncb * 128), KC, ncb * 128)
                for n in range(ncb):
                    f = 2 * i + n
                    e2 = f % 2
                    y4 = f % 4
                    Dg, DgR = take2()
                    fm_group(Dg, DgR, 128, lambda k: blkg[:, k, n * 128:(n + 1) * 128], aT, KC, colsA, gB, bgR + [aTR])
                    copy_op("act", gext[e2][:, 2:514], Dg[:, 0:512], DgR, [gextR[e2]])
                    if p == 0:
                        copy_op("dve", gext[e2][:, 0:2], Dg[:, 512:514], DgR, [gextR[e2]])
                        copy_op("dve", graw10[:, f, 2:10], Dg[:, 514:522], DgR, [graw10R])
                    else:
                        copy_op("dve", gext[e2][:, 0:2], ghist[:, f, :], [ghistR], [gextR[e2]])
                        copy_op("dve", graw10[:, f, 2:10], Dg[:, 512:520], DgR, [graw10R])
                    copy_op("dve", ghist[:, f, :], Dg[:, 510:512], DgR, [ghistR])
                    fw.op("act", lambda h: h.activation(out=yt[y4], in_=gext[e2][:, 2:514], func=AF.Identity, scale=convw[:, 2, f:f + 1], bias=convw[:, 3, f:f + 1]),
                          reads=[gextR[e2], cpkR], writes=[ytR[y4]])
                    fw.op("dve", lambda h: h.scalar_tensor_tensor(out=yt[y4], in0=gext[e2][:, 1:513], scalar=convw[:, 1, f:f + 1], in1=yt[y4],
                                                                    op0=ALU.mult, op1=ALU.add),
                          reads=[gextR[e2], ytR[y4], cpkR], writes=[ytR[y4]])
                    fw.op("dve", lambda h: h.scalar_tensor_tensor(out=yt[y4], in0=gext[e2][:, 0:512], scalar=convw[:, 0, f:f + 1], in1=yt[y4],
                                                                    op0=ALU.mult, op1=ALU.add),
                          reads=[gextR[e2], ytR[y4], cpkR], writes=[ytR[y4]])
                    fw.op("act", lambda h: h.activation(out=yt[y4], in_=yt[y4], func=AF.Silu), reads=[ytR[y4]], writes=[ytR[y4]])
                blku, buR = next_block(wview(w_up, 0, KC, DFF + i * 256, ncb * 128), KC, ncb * 128)
                for n in range(ncb):
                    f = 2 * i + n
                    y4 = f % 4
                    Du, DuR = take2()
                    fm_group(Du, DuR, 128, lambda k: blku[:, k, n * 128:(n + 1) * 128], aT, KC, colsA, uB, buR + [aTR])
                    copy_op("dve", us_raw[:, f, :], Du[:, 512:520], DuR, [usrR])
                    fw.op("dve", lambda h: h.tensor_tensor(out=actT[:, f, 0:512], in0=yt[y4], in1=Du[:, 0:512], op=ALU.mult),
                          reads=[ytR[y4]] + DuR, writes=[actTR])
            copy_op("dve", graw10[:, :, 0:2], ghist, [ghistR], [graw10R])
            stv = stT.rearrange("p f (s r) -> p f s r", r=2)
            gsr = graw10[:, :, 2:10]

            def bc(j):
                return convw[:, j, :].unsqueeze(2).to_broadcast([128, FC, 8])
            fw.op("dve", lambda h: h.tensor_tensor(out=tA, in0=gsr, in1=bc(2), op=ALU.mult), reads=[graw10R, cpkR], writes=[tAR])
            fw.op("dve", lambda h: h.tensor_tensor(out=tA, in0=tA, in1=bc(3), op=ALU.add), reads=[tAR, cpkR], writes=[tAR])
            fw.op("dve", lambda h: h.tensor_tensor(out=tB, in0=stv[:, :, :, 1], in1=bc(1), op=ALU.mult), reads=[stTR, cpkR], writes=[tBR])
            fw.op("dve", lambda h: h.tensor_tensor(out=tA, in0=tA, in1=tB, op=ALU.add), reads=[tAR, tBR], writes=[tAR])
            fw.op("dve", lambda h: h.tensor_tensor(out=tB, in0=stv[:, :, :, 0], in1=bc(0), op=ALU.mult), reads=[stTR, cpkR, tAR], writes=[tBR])
            fw.op("dve", lambda h: h.tensor_tensor(out=tA, in0=tA, in1=tB, op=ALU.add), reads=[tAR, tBR], writes=[tAR])
            fw.op("act", lambda h: h.activation(out=tA, in_=tA, func=AF.Silu), reads=[tAR], writes=[tAR])
            fw.op("dve", lambda h: h.tensor_tensor(out=actT[:, :, 512:520], in0=tA, in1=us_raw, op=ALU.mult), reads=[tAR, usrR], writes=[actTR])
            for i in range(NBLK):
                ncb = 4 if i < 10 else 3
                k2 = i % 2
                bk, bR = take1()
                for n in range(ncb):
                    fw.op("pe", lambda h: h.transpose(out=bk[0:10, n * 128:(n + 1) * 128], in_=graw10[:, 4 * i + n, :], identity=identf),
                          reads=[graw10R, cpkR], writes=bR, signal=(n == ncb - 1))
                copy_op(alt_eng(), grow[k2][:, 0:ncb * 128], bk[0:10, 0:ncb * 128], bR, [growR[k2]])
                fw.dma("sp", convs_d[s0:s0 + SPP, 1, i * 512:i * 512 + ncb * 128], grow[k2][2:10, 0:ncb * 128], reads=[growR[k2]], writes=[R_convs])
                if p == 1:
                    fw.dma("sp", convp_d[:, i * 512:i * 512 + ncb * 128], grow[k2][0:2, 0:ncb * 128], reads=[growR[k2]], writes=[R_convp])
            fw.dma("sp", convs_d[s0:s0 + SPP, 0, :], sconv[s0:s0 + SPP, 1, :], writes=[R_convs])
            dbg("actT", actT, [actTR], p)
            franges = [(0, 8), (8, 16), (16, 24), (24, 32), (32, 40), (40, 43)]
            for nb in range(4):
                banks = [take1() for _ in subs5]
                for (f0, f1) in franges:
                    nf = f1 - f0
                    blk, bkR = next_block(wview(w_down, f0 * 128, nf, nb * 512, 512), nf, 512)
                    for si, (slot, M, c0) in enumerate(subs5):
                        bk, bR = banks[si]
                        for fi in range(nf):
                            f = f0 + fi
                            mm(bk[0:M, :], actT[:, f, (512 if slot == 4 else c0):(512 if slot == 4 else c0) + M], blk[:, fi, :], f == 0, f == FC - 1, bkR + [actTR], bR, fi == nf - 1)
                for si, (slot, M, c0) in enumerate(subs5):
                    bk, bR = banks[si]
                    hv = hT[0:M, slot, nb * 512:(nb + 1) * 512]
                    fw.op("dve", lambda h: h.tensor_tensor(out=hv, in0=hv, in1=bk[0:M, :], op=ALU.add), reads=bR + [hR[slot]], writes=[hR[slot]])

            dbg("h2", hT, hR, p)
            norm_to_aT(subs5, 2)
            fw.new_phase()
            ar.reset(zmark)
            plT = ar.alloc([2, 520], BF16); plTR = fw.res("plT")
            pl_tm = ar.alloc([5, PLE], F32); plR = [fw.res(f"pl{i}") for i in range(5)]
            sg = ar.alloc([5, 512], F32); sgR = [fw.res(f"sg{i}") for i in range(5)]
            gfin = ar.alloc([D], F32); gfinR = fw.res("gfin")
            ybuf = [ar.alloc([D], F32) for _ in range(2)]; ybufR = [fw.res(f"ybuf{i}") for i in range(2)]
            fw.dma("sp", gfin, gfin_d.partition_broadcast(128), writes=[gfinR])
            for (slot, M, c0) in mains:
                fw.dma("sp", pl_tm[:, slot, :], pmd[p * 512 + slot * 128:p * 512 + slot * 128 + 128, :], writes=[plR[slot]])
            fw.dma("sp", pl_tm[0:SPP, 4, :], psd[s0:s0 + SPP, :], writes=[plR[4]])
            for (slot, M, c0) in subs5:
                bk, bR = take1()
                for c2 in range(2):
                    fw.op("pe", lambda h: h.transpose(out=bk[:, c2 * 128:c2 * 128 + M], in_=pl_tm[0:M, slot, c2 * 128:(c2 + 1) * 128],
                                                      identity=identf[0:M, 0:M]),
                          reads=[plR[slot], cpkR], writes=bR, signal=(c2 == 1))
                src = bk[:, 0:256].rearrange("p (c m) -> p c m", m=128)[:, :, 0:M]
                pc0 = 512 if slot == 4 else c0
                copy_op("act", plT[:, :, pc0:pc0 + M], src, bR, [plTR])
            for nb in range(4):
                tm_proj(w_pg, nb * 512, subs5, lambda k, c0, M: aT[:, k, c0:c0 + M], [aTR],
                        lambda slot, M, c0, bk, bR: fw.op("act", lambda h: h.activation(out=sg[0:M, slot, :], in_=bk[0:M, :], func=AF.Sigmoid),
                                                          reads=bR, writes=[sgR[slot]]))
                blk, bkR = next_block(wview(w_ple, 0, 2, nb * 512, 512), 2, 512)
                for (slot, M, c0) in subs5:
                    bk, bR = take1()
                    pc0 = 512 if slot == 4 else c0
                    tm_group(bk, bR, M, lambda k: plT[:, k, pc0:pc0 + M], lambda k: blk[:, k, :], 2, 512, bkR + [plTR])
                    fw.op("dve", lambda h: h.tensor_tensor(out=sg[0:M, slot, :], in0=bk[0:M, :], in1=sg[0:M, slot, :], op=ALU.mult),
                          reads=bR + [sgR[slot]], writes=[sgR[slot]])
                    hv = hT[0:M, slot, nb * 512:(nb + 1) * 512]
                    fw.op("dve", lambda h: h.tensor_tensor(out=hv, in0=hv, in1=sg[0:M, slot, :], op=ALU.add),
                          reads=[sgR[slot], hR[slot]], writes=[hR[slot]])
            dbg("h3", hT, hR, p)
            for si, (slot, M, c0) in enumerate(subs5):
                k2 = si % 2
                ss, ssR = stat_col()
                fw.op("act", lambda h: h.activation(out=ybuf[k2][0:M, :], in_=hT[0:M, slot, :], func=AF.Square, accum_out=ss[0:M, :]),
                      reads=[hR[slot]], writes=[ybufR[k2], ssR])
                rs, rsR = rstd_small(ss, ssR, M, D)
                fw.op("dve", lambda h: h.scalar_tensor_tensor(out=ybuf[k2][0:M, :], in0=hT[0:M, slot, :], scalar=rs[0:M, 0:1], in1=gfin[0:M, :],
                                                                op0=ALU.mult, op1=ALU.mult),
                      reads=[hR[slot], rsR, gfinR], writes=[ybufR[k2]])
                if slot < 4:
                    fw.dma("sp", y_d[p * 512 + slot * 128:p * 512 + slot * 128 + 128, :], ybuf[k2], reads=[ybufR[k2]], writes=[R_y])
                else:
                    fw.dma("sp", ys_d[s0:s0 + SPP, :], ybuf[k2][0:SPP, :], reads=[ybufR[k2]], writes=[R_ys])

        prefix_pass(0)
        prefix_pass(1)
        main_pass(0)
        main_pass(1)
        fw.finish(outs_res + dbg_list)

    fw.dry = True
    emit()
    fw.dry = False
    emit()
    return nc, fw


_CACHE = {}


def _host_consts(half):
    cpk = np.zeros((128, C_TOT), np.float32)
    cpk[:, C_ID:C_ID + 128] = np.eye(128, dtype=np.float32)
    s = np.arange(128)[:, None]
    t = np.arange(128)[None, :]
    cpk[:, C_UC:C_UC + 128] = np.where(s <= t, -1.0 / 16.0, 0.0)
    cpk[:, C_MK:C_MK + 128] = np.where(s <= t, 1.0, 0.0)
    cpk[:, C_OH:C_OH + 64] = np.eye(8, dtype=np.float32).reshape(1, 64)
    corr = np.ones((4, 16), np.float32)
    if half == 0:
        for g in range(4):
            wd = 2 ** (g + 1)
            for tt in range(16):
                corr[g, tt] = wd / min(tt + 1, wd)
    cpk[:, C_CORR:C_CORR + 64] = corr.reshape(1, 64)
    return cpk


def kernel(x_prompt, x_sample, state_gla, state_pool, state_conv, p_prompt, p_sample,
           norm_mix, w_in, w_gate_up, b_gate, gla_norm, w_branch_a, w_pool, pool_scale,
           w_branch_b, w_out, norm_ffn, w_up, conv_w, conv_b, w_down, norm_ple, w_ple_gate,
           w_ple, norm_final):
    f = lambda a: np.ascontiguousarray(np.asarray(a, dtype=np.float32))
    x_prompt, x_sample = f(x_prompt), f(x_sample)
    state_gla, state_pool, state_conv = f(state_gla), f(state_pool), f(state_conv)
    p_prompt, p_sample = f(p_prompt), f(p_sample)
    if "nc" not in _CACHE:
        _CACHE["nc"] = build_program()
    nc, fw = _CACHE["nc"]

    def fm(v, nch):
        return f(v).reshape(nch, 128).T
    base = {}
    for half in (0, 1):
        cpk = _host_consts(half)
        cpk[:, C_GT + 0:C_GT + 16] = fm(norm_mix[0], 16)
        cpk[:, C_GT + 16:C_GT + 32] = fm(norm_ffn[0], 16)
        cpk[:, C_GT + 32:C_GT + 48] = fm(norm_ple[0], 16)
        cpk[:, C_GN:C_GN + 16] = fm(gla_norm[0], 16)
        cpk[:, C_PS:C_PS + 8] = fm(pool_scale[0], 8)
        for j in range(3):
            cpk[:, C_CW + j * FC:C_CW + (j + 1) * FC] = fm(conv_w[0, j], FC)
        cpk[:, C_CW + 3 * FC:C_CW + 4 * FC] = fm(conv_b[0], FC)
        base[half] = cpk
    selw = np.zeros((120, 32), np.float32)
    for g in range(4):
        wd = 2 ** (g + 1)
        for s in range(8):
            for r in range(15 - (wd - 1), 15):
                selw[s * 15 + r, g * 8 + s] = 1.0
    wg_aug = np.concatenate([f(w_gate_up[0]), f(b_gate[0])[None, :]], axis=0)
    shared = {
        "w_in": f(w_in[0]), "wg_aug": wg_aug, "w_a": f(w_branch_a[0]), "w_pool": f(w_pool[0]),
        "w_b": f(w_branch_b[0]), "w_out": f(w_out[0]), "w_up": f(w_up[0]), "w_down": f(w_down[0]),
        "w_pg": f(w_ple_gate[0]), "w_ple": f(w_ple[0]), "gfin": f(norm_final), "selw": selw,
    }
    in_maps = []
    for c in range(8):
        b, half = c // 2, c % 2
        m = dict(shared)
        m["cpk"] = base[half]
        m["xm"] = x_prompt[b, half * 1024:(half + 1) * 1024]
        if half == 1:
            m["xpre"] = x_prompt[b, 0:NPRE]
            m["xh"] = x_prompt[b, NPRE:1024]
        else:
            m["xpre"] = np.zeros((NPRE, D), np.float32)
            m["xh"] = np.zeros((HALO, D), np.float32)
        m["xs"] = x_sample[c * 16:(c + 1) * 16, 0]
        m["pm"] = p_prompt[0, b, half * 1024:(half + 1) * 1024]
        m["ps"] = p_sample[0, c * 16:(c + 1) * 16, 0]
        m["sgla"] = state_gla[0, c * 16:(c + 1) * 16]
        m["spool"] = state_pool[0, c * 16:(c + 1) * 16]
        m["sconv"] = state_conv[0, c * 16:(c + 1) * 16]
        in_maps.append({k: np.ascontiguousarray(v) for k, v in m.items()})
    res = run_bass_kernel_spmd(nc, in_maps, core_ids=list(range(8)))
    R = res.results
    if DEBUG:
        _CACHE["raw"] = R
    B = x_prompt.shape[0]
    y_prompt = np.zeros((B, 2048, D), np.float32)
    y_sample = np.zeros((128, 1, D), np.float32)
    gla_p = np.zeros((1, B, NH, DK, DV), np.float32)
    pool_p = np.zeros((1, B, 15, 1024), np.float32)
    conv_p = np.zeros((1, B, 2, DFF), np.float32)
    gla_s = np.zeros((1, 128, NH, DK, DV), np.float32)
    pool_s = np.zeros((1, 128, 15, 1024), np.float32)
    conv_s = np.zeros((1, 128, 2, DFF), np.float32)
    for c in range(8):
        b, half = c // 2, c % 2
        r = R[c]
        y_prompt[b, half * 1024:(half + 1) * 1024] = r["y"]
        y_sample[c * 16:(c + 1) * 16, 0] = r["ys"]
        gla_s[0, c * 16:(c + 1) * 16] = r["gla_s"]
        pool_s[0, c * 16:(c + 1) * 16] = r["pool_s"]
        conv_s[0, c * 16:(c + 1) * 16] = r["conv_s"]
        if half == 1:
            gla_p[0, b] = r["gla_p"]
            pool_p[0, b] = r["pool_p"]
            conv_p[0, b] = r["conv_p"]
    return (y_prompt, y_sample, gla_p, pool_p, conv_p, gla_s, pool_s, conv_s)
```

```python
import numpy as np
from contextlib import ExitStack
import concourse.bass as bass
import concourse.mybir as mybir
from concourse.bass_utils import run_bass_kernel_spmd

F32 = mybir.dt.float32
BF16 = mybir.dt.bfloat16
U8 = mybir.dt.uint8
AF = mybir.ActivationFunctionType
ALU = mybir.AluOpType

D = 2048
KC = 16
DFF = 5504
FC = 43
NH = 4
DK = 256
DV = 512
PLE = 256
OQ, OK_, OV, OR, OGLR, OU, OGA, OGB = 0, 1024, 2048, 4096, 6144, 6160, 7184, 9232
INW = 11280
EPS = 1e-6
HALO = 32
NPRE = 992
PRE_SUBS = [[128, 128, 128, 128], [128, 128, 128, 96]]
NSAMP = 16
SPP = 8
SEM_LIMIT = 30000
DEBUG = False
DEBUG_PASS = 0
RING_NB = 4
RING_ELEMS = 4096

C_ID, C_UC, C_MK, C_OH, C_CORR, C_GT, C_GN, C_PS, C_CW = 0, 128, 256, 384, 448, 512, 560, 576, 584
C_TOT = 584 + 172


class Res:
    __slots__ = ("name", "w", "r")

    def __init__(self, name="", init=None):
        self.name = name
        self.w = dict(init) if init else {}
        self.r = {}


class Eng:
    def __init__(self, name, h):
        self.name = name
        self.h = h
        self.sem = None
        self.count = 0
        self.known = {}
        self.pending = False


class FW:
    def __init__(self, nc):
        self.nc = nc
        self.stack = ExitStack()
        self.sems = {}
        self.hist = {}
        self.nsem = 0
        self.E = {}
        self.dry = False
        for name, h in (("pe", nc.tensor), ("act", nc.scalar), ("dve", nc.vector),
                        ("pool", nc.gpsimd), ("sp", nc.sync)):
            e = Eng(name, h)
            self.E[name] = e
            self._new_eng_sem(e)
        self.dma_pool = {}
        self.zclock = {}
        self.n_wait = 0
        self.n_ins = 0

    def new_sem(self, name):
        self.nsem += 1
        key = f"{name}_{self.nsem}"
        h = self.stack.enter_context(self.nc.semaphore(key))
        self.sems[key] = h
        return key

    def _new_eng_sem(self, e):
        e.sem = self.new_sem("s" + e.name)
        e.count = 0

    def make_dma_pool(self, qname, n):
        self.dma_pool[qname] = {"keys": [self.new_sem(f"d{qname}") for _ in range(n)],
                                "vals": [0] * n, "i": 0}

    def res(self, name=""):
        return Res(name, self.zclock)

    def new_phase(self):
        if self.dry:
            return
        clk = {}
        for e in self.E.values():
            if e.pending:
                raise RuntimeError("pending op at phase boundary " + e.name)
            if e.count > 0:
                clk[e.sem] = e.count
        for pool in self.dma_pool.values():
            for k, v in zip(pool["keys"], pool["vals"]):
                if v > 0:
                    clk[k] = v
        self.zclock = clk

    def _wait(self, e, need, own_raw):
        for key, val in need.items():
            if key == e.sem and not own_raw.get(key):
                continue
            if e.known.get(key, 0) >= val:
                continue
            e.h.wait_ge(self.sems[key], val)
            self.n_wait += 1
            snap = self.hist.get((key, val))
            if snap:
                for k2, v2 in snap.items():
                    if e.known.get(k2, 0) < v2:
                        e.known[k2] = v2
            e.known[key] = val

    def _deps(self, e, reads, writes):
        need = {}
        own_raw = {}
        for r in reads:
            for k, v in r.w.items():
                if need.get(k, 0) < v:
                    need[k] = v
                if k == e.sem:
                    own_raw[k] = True
        for w in writes:
            for d in (w.w, w.r):
                for k, v in d.items():
                    if need.get(k, 0) < v:
                        need[k] = v
        return need, own_raw

    def _record(self, key, val, reads, writes):
        for r in reads:
            if r.r.get(key, 0) < val:
                r.r[key] = val
        for w in writes:
            w.w = {key: val}
            w.r = {}

    def op(self, eng, fn, reads=(), writes=(), signal=True):
        if self.dry:
            return None
        e = self.E[eng]
        need, own_raw = self._deps(e, reads, writes)
        self._wait(e, need, own_raw)
        ins = fn(e.h)
        self.n_ins += 1
        if signal:
            if e.count >= SEM_LIMIT:
                if e.pending:
                    raise RuntimeError("sem rollover with pending ops")
                self._new_eng_sem(e)
            e.count += 1
            ins.then_inc(self.sems[e.sem], 1)
            val = e.count
            self.hist[(e.sem, val)] = dict(e.known)
            e.pending = False
        else:
            if e.count + 1 > SEM_LIMIT:
                raise RuntimeError("unsignaled op at sem rollover")
            val = e.count + 1
            e.pending = True
        self._record(e.sem, val, reads, writes)
        return ins

    def dma(self, q, out, in_, reads=(), writes=(), **kw):
        if self.dry:
            return None
        e = self.E[q]
        pool = self.dma_pool[q]
        i = pool["i"]
        pool["i"] = (i + 1) % len(pool["keys"])
        key = pool["keys"][i]
        need, own_raw = self._deps(e, reads, writes)
        if pool["vals"][i] > 0:
            need[key] = max(need.get(key, 0), pool["vals"][i])
        self._wait(e, need, own_raw)
        ins = e.h.dma_start(out=out, in_=in_, **kw)
        self.n_ins += 1
        pool["vals"][i] += 16
        val = pool["vals"][i]
        ins.then_inc(self.sems[key], 16)
        self.hist[(key, val)] = dict(e.known)
        self._record(key, val, reads, writes)
        return ins

    def finish(self, outs):
        if self.dry:
            return
        e = self.E["sp"]
        need = {}
        for r in outs:
            for k, v in r.w.items():
                if need.get(k, 0) < v:
                    need[k] = v
        for en in self.E.values():
            if en.pending:
                raise RuntimeError(f"engine {en.name} has pending unsignaled ops")
            if en.count > 0 and en is not e:
                need[en.sem] = max(need.get(en.sem, 0), en.count)
        for pool in self.dma_pool.values():
            for k, v in zip(pool["keys"], pool["vals"]):
                if v > 0:
                    need[k] = max(need.get(k, 0), v)
        self._wait(e, need, {})

    def close(self):
        self.stack.close()


class Arena:
    def __init__(self, nc, nbytes):
        self.t = nc.alloc_sbuf_tensor("arena", [128, nbytes], U8)
        self.nbytes = nbytes
        self.top = 0
        self.marks = []

    def alloc(self, shape, dt, parts=128):
        esz = 4 if dt == F32 else 2
        n = 1
        for s in shape:
            n *= s
        nb = (n * esz + 31) // 32 * 32
        off = self.top
        if off + nb > self.nbytes:
            raise RuntimeError(f"arena overflow: need {off + nb} > {self.nbytes}")
        self.top = off + nb
        v = self.t[0:parts, off:off + n * esz].bitcast(dt)
        if len(shape) == 2:
            v = v.rearrange("p (a b) -> p a b", b=shape[1])
        elif len(shape) == 3:
            v = v.rearrange("p (a b c) -> p a b c", b=shape[1], c=shape[2])
        return v

    def mark(self):
        return self.top

    def reset(self, m):
        self.top = m


def build_program():
    nc = bass.Bass("TRN2", target_bir_lowering=False)

    def din(name, shape):
        return nc.dram_tensor(name, list(shape), F32, kind="ExternalInput").ap()

    def dout(name, shape):
        return nc.dram_tensor(name, list(shape), F32, kind="ExternalOutput").ap()

    xm = din("xm", [1024, D]); xh = din("xh", [HALO, D]); xpre = din("xpre", [NPRE, D]); xs = din("xs", [NSAMP, D])
    pmd = din("pm", [1024, PLE]); psd = din("ps", [NSAMP, PLE])
    sgla = din("sgla", [NSAMP, NH, DK, DV]); spool = din("spool", [NSAMP, 15, 1024]); sconv = din("sconv", [NSAMP, 2, DFF])
    w_in = din("w_in", [D, INW]); wg_d = din("wg_aug", [17, 1024]); w_a = din("w_a", [D, D]); w_pool = din("w_pool", [4, 256, 256])
    w_b = din("w_b", [1024, D]); w_out = din("w_out", [D, D]); w_up = din("w_up", [D, 2 * DFF]); w_down = din("w_down", [DFF, D])
    w_pg = din("w_pg", [D, D]); w_ple = din("w_ple", [PLE, D]); gfin_d = din("gfin", [D])
    cpk_d = din("cpk", [128, C_TOT]); selw_d = din("selw", [120, 32])
    y_d = dout("y", [1024, D]); ys_d = dout("ys", [NSAMP, D])
    glap_d = dout("gla_p", [NH, DK, DV]); poolp_d = dout("pool_p", [15, 1024]); convp_d = dout("conv_p", [2, DFF])
    glas_d = dout("gla_s", [NSAMP, NH, DK, DV]); pools_d = dout("pool_s", [NSAMP, 15, 1024]); convs_d = dout("conv_s", [NSAMP, 2, DFF])

    fw = FW(nc)
    dbg_list = []

    def dbg(name, view, reads, p):
        if not DEBUG or p != DEBUG_PASS or fw.dry:
            return
        shp = list(view.shape)
        t = nc.dram_tensor("dbg_" + name, shp, view.dtype, kind="ExternalOutput").ap()
        r_ = Res("dbg_" + name)
        fw.dma("sp", t, view, reads=reads, writes=[r_])
        dbg_list.append(r_)
    fw.make_dma_pool("sp", 12)
    fw.make_dma_pool("pool", 4)
    outs_res = [Res("o_" + n) for n in ("y", "ys", "glap", "poolp", "convp", "glas", "pools", "convs")]
    R_y, R_ys, R_glap, R_poolp, R_convp, R_glas, R_pools, R_convs = outs_res

    ar = Arena(nc, (nc.sbuf_bytes_remaining - 64) // 256 * 256)
    Dps = [nc.alloc_psum_tensor(f"D{i}", [128, 1024], F32) for i in range(4)]
    bankR = [Res(f"bank{i}") for i in range(8)]

    def bank_ap(b):
        return Dps[b // 2][:, (b % 2) * 512:(b % 2) * 512 + 512]

    st = {"pp": 0, "cur": 0, "alt": 0}

    def take1():
        b = st["pp"]
        st["pp"] = (b + 1) % 6
        return bank_ap(b), [bankR[b]]

    def take2():
        b = st["pp"]
        if b % 2:
            b = (b + 1) % 6
        st["pp"] = (b + 2) % 6
        return Dps[b // 2], [bankR[b], bankR[b + 1]]

    def bf16v(ap512):
        return ap512.bitcast(BF16)

    def alt_eng():
        st["alt"] ^= 1
        return "act" if st["alt"] else "dve"

    def copy_op(eng, out, in_, reads, writes):
        if eng == "act":
            fw.op("act", lambda h: h.copy(out=out, in_=in_), reads=reads, writes=writes)
        else:
            fw.op(eng, lambda h: h.tensor_copy(out=out, in_=in_), reads=reads, writes=writes)

    ring = [ar.alloc([RING_ELEMS], BF16) for _ in range(RING_NB)]
    ringR = [Res(f"ring{i}") for i in range(RING_NB)]
    hT = ar.alloc([6, D], F32)
    hR = [Res(f"h{i}") for i in range(6)]
    aT = ar.alloc([KC, 552], BF16); aTR = Res("aT")
    S_ = ar.alloc([NH * 2, DV], F32); SR = [Res(f"S{i}") for i in range(NH * 2)]
    Sbf = ar.alloc([NH * 2, DV], BF16); SbfR = [Res(f"Sbf{i}") for i in range(NH * 2)]
    a_tm = ar.alloc([D], BF16); a_tmR = Res("a_tm")
    cpk = ar.alloc([C_TOT], F32); cpkR = Res("cpk")
    ident_bf = ar.alloc([128], BF16); identbR = Res("identb")
    wg = ar.alloc([1024], F32, parts=17); wgR = Res("wg")
    stats = ar.alloc([64], F32)
    statR = [Res(f"stat{i}") for i in range(64)]
    uhist = ar.alloc([8, 15], F32); uhistR = Res("uhist")
    graw10 = ar.alloc([FC, 10], F32); graw10R = Res("graw10")
    ghist = ar.alloc([FC, 2], F32); ghistR = Res("ghist")
    zmark = ar.mark()

    identf = cpk[:, C_ID:C_ID + 128]
    Ucum = cpk[:, C_UC:C_UC + 128]
    mask01 = cpk[:, C_MK:C_MK + 128]
    oh_row = cpk[:, C_OH:C_OH + 64].rearrange("p (s t) -> p s t", t=8)
    corr = cpk[:, C_CORR:C_CORR + 64].rearrange("p (g t) -> p g t", t=16)
    gT = cpk[:, C_GT:C_GT + 48].rearrange("p (i c) -> p i c", c=16)
    gnormT = cpk[:, C_GN:C_GN + 16]
    pscale = cpk[:, C_PS:C_PS + 8]
    convw = cpk[:, C_CW:C_CW + 172].rearrange("p (j f) -> p j f", f=FC)

    stc = {"i": 0}

    def stat_col():
        i = stc["i"]
        stc["i"] = (i + 1) % 64
        return stats[:, i:i + 1], statR[i]

    plan = []

    def issue_block(i):
        if i >= len(plan):
            return
        view, kc, ncols = plan[i]
        slot = i % RING_NB
        dst = ring[slot][:, 0:kc * ncols].rearrange("p (k n) -> p k n", n=ncols)
        fw.dma("pool", dst, view, writes=[ringR[slot]])

    def next_block(view, kc, ncols):
        i = st["cur"]
        st["cur"] = i + 1
        if fw.dry:
            plan.append((view, kc, ncols))
        else:
            issue_block(i + RING_NB - 1)
        slot = i % RING_NB
        return ring[slot][:, 0:kc * ncols].rearrange("p (k n) -> p k n", n=ncols), [ringR[slot]]

    def wview(w, r0, nk, c0, ncols):
        return w[r0:r0 + nk * 128, c0:c0 + ncols].rearrange("(k p) n -> p k n", p=128)

    def mm(out, lhsT, rhs, start, stop, reads, writes, signal):
        fw.op("pe", lambda h: h.matmul(out, lhsT=lhsT, rhs=rhs, start=start, stop=stop),
              reads=reads, writes=writes, signal=signal)

    def fm_group(Dt, DR, M, lhs_of_k, rhsT, nk, colsA, colsB, rd):
        a0, a1 = colsA
        for k in range(nk):
            mm(Dt[0:M, 0:a1 - a0], lhs_of_k(k), rhsT[:, k, a0:a1], k == 0, k == nk - 1, rd, DR,
               signal=(k == nk - 1) and colsB is None)
        if colsB is not None:
            b0, b1 = colsB
            for k in range(nk):
                mm(Dt[0:M, 512:512 + b1 - b0], lhs_of_k(k), rhsT[:, k, b0:b1], k == 0, k == nk - 1, rd, DR,
                   signal=(k == nk - 1))

    def tm_proj(w, c0w, subs, lhsT_of, lhsR, evac, pump_fn=None):
        banks = [take1() for _ in subs]
        for kh in range(2):
            blk, bkR = next_block(wview(w, kh * 8 * 128, 8, c0w, 512), 8, 512)
            for si, (slot, M, c0) in enumerate(subs):
                bk, bR = banks[si]
                for k in range(8):
                    mm(bk[0:M, 0:512], lhsT_of(kh * 8 + k, c0, M), blk[:, k, :], kh == 0 and k == 0, kh == 1 and k == 7,
                       bkR + lhsR, bR, signal=(k == 7))
                if pump_fn is not None:
                    pump_fn()
        for si, (slot, M, c0) in enumerate(subs):
            bk, bR = banks[si]
            evac(slot, M, c0, bk, bR)

    def tm_group(bank, bR, M, lhsT_of_k, rhs_of_k, nk, ncols, rd, first=True, last=True):
        for k in range(nk):
            mm(bank[0:M, 0:ncols], lhsT_of_k(k), rhs_of_k(k), first and k == 0, last and k == nk - 1, rd, bR,
               signal=(k == nk - 1))

    def rstd_from(ss, ssR, n):
        lnv, lnR = stat_col()
        rs, rsR = stat_col()
        return lnv, lnR, rs, rsR

    def norm_to_aT(subs, gi):
        for (slot, M, c0) in subs:
            ss, ssR = stat_col()
            lnv, lnR = stat_col()
            rs, rsR = stat_col()
            fw.op("act", lambda h: h.activation(out=a_tm[0:M, :], in_=hT[0:M, slot, :], func=AF.Square, accum_out=ss[0:M, :]),
                  reads=[hR[slot]], writes=[a_tmR, ssR])
            fw.op("act", lambda h: h.activation(out=lnv[0:M, :], in_=ss[0:M, :], func=AF.Ln, scale=1.0 / D, bias=EPS),
                  reads=[ssR], writes=[lnR])
            fw.op("act", lambda h: h.activation(out=rs[0:M, :], in_=lnv[0:M, :], func=AF.Exp, scale=-0.5),
                  reads=[lnR], writes=[rsR])
            fw.op("dve", lambda h: h.tensor_scalar(out=a_tm[0:M, :], in0=hT[0:M, slot, :], scalar1=rs[0:M, 0:1], scalar2=None, op0=ALU.mult),
                  reads=[hR[slot], rsR], writes=[a_tmR])
            Dt, DR = take2()
            pv = Dt[:, :].bitcast(BF16).rearrange("p (c m) -> p c m", m=128)
            for c in range(KC):
                fw.op("pe", lambda h: h.transpose(out=pv[:, c, 0:M], in_=a_tm[0:M, c * 128:(c + 1) * 128], identity=ident_bf[0:M, 0:M]),
                      reads=[a_tmR, identbR], writes=DR, signal=(c == KC - 1))
            gb = gT[:, gi, :].unsqueeze(2).to_broadcast([128, KC, M])
            fw.op("dve", lambda h: h.tensor_tensor(out=aT[:, :, c0:c0 + M], in0=pv[:, :, 0:M], in1=gb, op=ALU.mult),
                  reads=DR + [cpkR], writes=[aTR])

    def rstd_small(ss, ssR, M, n):
        lnv, lnR = stat_col()
        rs, rsR = stat_col()
        fw.op("act", lambda h: h.activation(out=lnv[0:M, :], in_=ss[0:M, :], func=AF.Ln, scale=1.0 / n, bias=EPS),
              reads=[ssR], writes=[lnR])
        fw.op("act", lambda h: h.activation(out=rs[0:M, :], in_=lnv[0:M, :], func=AF.Exp, scale=-0.5),
              reads=[lnR], writes=[rsR])
        return rs, rsR

    def emit():
        st["pp"] = 0; st["cur"] = 0; st["alt"] = 0; stc["i"] = 0
        ar.reset(zmark)
        if not fw.dry:
            for i in range(RING_NB - 1):
                issue_block(i)
        fw.dma("sp", cpk, cpk_d, writes=[cpkR])
        fw.dma("sp", wg, wg_d, writes=[wgR])
        fw.op("dve", lambda h: h.tensor_copy(out=ident_bf, in_=identf), reads=[cpkR], writes=[identbR])
        for hh in range(NH):
            fw.op("dve", lambda h: h.memset(S_[:, 2 * hh:2 * hh + 2, :], 0.0), writes=[SR[2 * hh], SR[2 * hh + 1]])
        fw.op("dve", lambda h: h.memset(uhist, 0.0), writes=[uhistR])
        fw.op("dve", lambda h: h.memset(ghist, 0.0), writes=[ghistR])

        def gla_decay(hh, tsubs, glr_aug, glrR, g1s, bfm, bfmR, Ah, AhR):
            for ti, (slot, M, c0) in enumerate(tsubs):
                g1, g1R = g1s[ti]
                bk, bR = take1()
                mm(bk[0:M, 0:256], glr_aug[:, c0:c0 + M], wg[:, hh * 256:(hh + 1) * 256], True, True, [glrR, wgR], bR, True)
                fw.op("act", lambda h: h.activation(out=g1[0:M, :], in_=bk[0:M, 0:256], func=AF.Exp, scale=-1.0),
                      reads=bR, writes=[g1R])
                fw.op("act", lambda h: h.activation(out=g1[0:M, :], in_=g1[0:M, :], func=AF.Ln, bias=1.0),
                      reads=[g1R], writes=[g1R])
            for ti, (slot, M, c0) in enumerate(tsubs):
                g1, g1R = g1s[ti]
                bk2, bR2 = take1()
                for c in range(2):
                    mm(bk2[:, c * 128:c * 128 + M], g1[0:M, c * 128:(c + 1) * 128], Ucum[0:M, 0:M], True, True,
                       [g1R, cpkR], bR2, c == 1)
                src = bk2[:, 0:256].rearrange("p (c m) -> p c m", m=128)[:, :, 0:M]
                copy_op("dve", bfm[:, :, c0:c0 + M], src, bR2, [bfmR])
                fw.op("act", lambda h: h.activation(out=Ah[:, :, ti:ti + 1], in_=bfm[:, :, c0 + M - 1:c0 + M], func=AF.Exp),
                      reads=[bfmR], writes=[AhR])

        def s_update(hh, ti, M, kd_tm, kdR, vsl, vR, Ah, AhR, cast):
            for c in range(2):
                bk, bR = take1()
                mm(bk[:, :], kd_tm[0:M, c * 128:(c + 1) * 128], vsl, True, True, [kdR, vR], bR, True)
                Sv = S_[:, 2 * hh + c, :]
                fw.op("dve", lambda h: h.scalar_tensor_tensor(out=Sv, in0=Sv, scalar=Ah[:, c, ti:ti + 1], in1=bk[:, :],
                                                                op0=ALU.mult, op1=ALU.add),
                      reads=bR + [AhR, SR[2 * hh + c]], writes=[SR[2 * hh + c]])
            if cast:
                fw.op("act", lambda h: h.copy(out=Sbf[:, 2 * hh:2 * hh + 2, :], in_=S_[:, 2 * hh:2 * hh + 2, :]),
                      reads=[SR[2 * hh], SR[2 * hh + 1]], writes=[SbfR[2 * hh], SbfR[2 * hh + 1]])

        def kd_transpose(M, c0, kdT, kdTR, kd_tm, kdR):
            bk, bR = take1()
            bv = bf16v(bk)
            for c in range(2):
                fw.op("pe", lambda h: h.transpose(out=bv[0:M, c * 128:(c + 1) * 128], in_=kdT[:, c, c0:c0 + M], identity=ident_bf),
                      reads=[kdTR, identbR], writes=bR, signal=(c == 1))
            copy_op("act", kd_tm[0:M, :], bv[0:M, 0:256], bR, [kdR])

        def prefix_pass(pi):
            fw.new_phase()
            ar.reset(zmark)
            sizes = PRE_SUBS[pi]
            subs = []
            c0 = 0
            for j, M in enumerate(sizes):
                subs.append((j, M, c0))
                c0 += M
            NP = c0
            tok0 = sum(sum(s) for s in PRE_SUBS[:pi])
            for (slot, M, cc) in subs:
                fw.dma("sp", hT[0:M, slot, :], xpre[tok0 + cc:tok0 + cc + M, :], writes=[hR[slot]])
            norm_to_aT(subs, 0)
            glr_aug = ar.alloc([552], F32, parts=17); glrR = fw.res("glr")
            g1s = [(ar.alloc([256], F32), fw.res(f"g1_{i}")) for i in range(4)]
            bfm = ar.alloc([2, 544], F32); bfmR = fw.res("bfm")
            Ah = ar.alloc([2, 8], F32); AhR = fw.res("Ah")
            Etmp = ar.alloc([544], F32); EtR = fw.res("Et")
            kdT = ar.alloc([2, 552], BF16); kdTR = fw.res("kdT")
            vh = ar.alloc([6, DV], BF16); vR = fw.res("vh")
            kd_tm = ar.alloc([256], BF16); kdR = fw.res("kdtm")
            fw.op("dve", lambda h: h.memset(glr_aug, 1.0), writes=[glrR])
            blk, bkR = next_block(wview(w_in, 0, KC, OGLR, 16), KC, 16)
            Dt, DR = take2()
            fm_group(Dt, DR, 16, lambda k: blk[:, k, 0:16], aT, KC, (0, NP), None, bkR + [aTR])
            copy_op("act", glr_aug[0:16, 0:NP], Dt[0:16, 0:NP], DR, [glrR])
            for hh in range(NH):
                gla_decay(hh, subs, glr_aug, glrR, g1s, bfm, bfmR, Ah, AhR)
                blk, bkR = next_block(wview(w_in, 0, KC, OK_ + hh * 256, 256), KC, 256)
                for c in range(2):
                    Dt, DR = take2()
                    fm_group(Dt, DR, 128, lambda k: blk[:, k, c * 128:(c + 1) * 128], aT, KC, (0, NP), None, bkR + [aTR])
                    for (slot, M, cc) in subs:
                        fw.op("act", lambda h: h.activation(out=Etmp[:, cc:cc + M], in_=bfm[:, c, cc:cc + M], func=AF.Exp, scale=-1.0,
                                                              bias=bfm[:, c, cc + M - 1:cc + M]),
                              reads=[bfmR], writes=[EtR])
                    fw.op("dve", lambda h: h.tensor_tensor(out=kdT[:, c, 0:NP], in0=Dt[:, 0:NP], in1=Etmp[:, 0:NP], op=ALU.mult),
                          reads=DR + [EtR], writes=[kdTR])
                tm_proj(w_in, OV + hh * 512, subs, lambda k, c0, M: aT[:, k, c0:c0 + M], [aTR],
                        lambda slot, M, c0, bk, bR: copy_op(alt_eng(), vh[0:M, slot, :], bk[0:M, :], bR, [vR]))
                for ti, (slot, M, cc) in enumerate(subs):
                    kd_transpose(M, cc, kdT, kdTR, kd_tm, kdR)
                    s_update(hh, ti, M, kd_tm, kdR, vh[0:M, slot, :], vR, Ah, AhR, cast=(pi == 1 and ti == len(subs) - 1))

        def main_pass(p):
            fw.new_phase()
            ar.reset(zmark)
            if p == 0:
                halo_c0, samp_c0, NC = 512, 544, 552
            else:
                halo_c0, samp_c0, NC = None, 512, 520
            NP = samp_c0
            colsA, colsB = (0, 512), (512, NC)
            mains = [(j, 128, j * 128) for j in range(4)]
            samp = (4, SPP, samp_c0)
            halo = (5, HALO, halo_c0) if p == 0 else None
            allsubs = mains + [samp] + ([halo] if halo else [])
            subs5 = mains + [samp]
            tsubs = ([halo] if halo else []) + mains
            s0 = p * SPP
            if p == 0:
                for (slot, M, c0) in mains:
                    fw.dma("sp", hT[:, slot, :], xm[p * 512 + slot * 128:p * 512 + slot * 128 + 128, :], writes=[hR[slot]])
                fw.dma("sp", hT[0:SPP, 4, :], xs[s0:s0 + SPP, :], writes=[hR[4]])
            if halo:
                fw.dma("sp", hT[0:HALO, 5, :], xh, writes=[hR[5]])
            norm_to_aT(allsubs, 0)
            dbg("aT", aT, [aTR], p)

            ogT = ar.alloc([KC, 552], BF16); ogTR = fw.res("ogT")
            z1a = ar.mark()
            glr_aug = ar.alloc([552], F32, parts=17); glrR = fw.res("glr")
            g1 = ar.alloc([256], F32); g1R = fw.res("g1")
            bfm = ar.alloc([2, 544], F32); bfmR = fw.res("bfm")
            Ah = ar.alloc([2, 8], F32); AhR = fw.res("Ah")
            Et = [ar.alloc([544], F32) for _ in range(2)]; EtR = [fw.res(f"Et{i}") for i in range(2)]
            g1s = [(g1, g1R), (Et[0][:, 0:256], EtR[0]), (Et[0][:, 272:528], EtR[0]), (Et[1][:, 0:256], EtR[1]), (Et[1][:, 272:528], EtR[1])]
            qbT = ar.alloc([2, 552], BF16); qbTR = fw.res("qbT")
            kbT = ar.alloc([2, 552], BF16); kbTR = fw.res("kbT")
            kdT = ar.alloc([2, 552], BF16); kdTR = fw.res("kdT")
            vh = ar.alloc([5, DV], BF16); vR = fw.res("vh")
            gr = ar.alloc([5, DV], BF16); grR = [fw.res(f"gr{i}") for i in range(5)]
            vs = [ar.alloc([DV], BF16, parts=8) for _ in range(2)]; vsR = [fw.res(f"vs{i}") for i in range(2)]
            grs = [ar.alloc([DV], BF16, parts=8) for _ in range(2)]; grsR = [fw.res(f"grs{i}") for i in range(2)]
            ks_sb = [ar.alloc([256], F32, parts=8) for _ in range(2)]; ksbR = [fw.res(f"ksb{i}") for i in range(2)]
            att_sb = [ar.alloc([128], BF16) for _ in range(2)]; attR = [fw.res(f"att{i}") for i in range(2)]
            kd_tm = [ar.alloc([256], BF16) for _ in range(2)]; kdR = [fw.res(f"kdtm{i}") for i in range(2)]
            qsT = ar.alloc([8, 8], F32); qsTR = fw.res("qsT")
            ksT = ar.alloc([8, 8], F32); ksTR = fw.res("ksT")
            aTs = ar.alloc([8, 8], F32); aTsR = fw.res("aTs")
            etmp_s = ar.alloc([8, 8], F32); etsR = fw.res("etmps")
            kmask = ar.alloc([2, 256], BF16, parts=8); kmR = [fw.res(f"kmask{i}") for i in range(2)]
            qmask = [ar.alloc([2, 8, 8], BF16) for _ in range(2)]; qmR = [fw.res(f"qmask{i}") for i in range(2)]
            Sin = [ar.alloc([DV], F32) for _ in range(3)]; SinR = [fw.res(f"Sin{i}") for i in range(3)]
            Snew = [ar.alloc([DV], F32) for _ in range(3)]; SnewR = [fw.res(f"Snew{i}") for i in range(3)]
            Snbf = [ar.alloc([DV], BF16) for _ in range(3)]; SnbfR = [fw.res(f"Snbf{i}") for i in range(3)]

            fw.op("dve", lambda h: h.memset(glr_aug, 1.0), writes=[glrR])
            blk, bkR = next_block(wview(w_in, 0, KC, OGLR, 16), KC, 16)
            Dt, DR = take2()
            fm_group(Dt, DR, 16, lambda k: blk[:, k, 0:16], aT, KC, colsA, colsB, bkR + [aTR])
            copy_op("act", glr_aug[0:16, 0:NC], Dt[0:16, 0:NC], DR, [glrR])
            bk, bR = take1()
            for c in range(8):
                mm(bk[:, c * 8:(c + 1) * 8], wg[:, c * 128:(c + 1) * 128], glr_aug[:, samp_c0:samp_c0 + SPP], True, True,
                   [wgR, glrR], bR, c == 7)
            zsv = bk[:, 0:64].rearrange("p (c s) -> p c s", s=8)
            fw.op("act", lambda h: h.activation(out=etmp_s, in_=zsv, func=AF.Exp, scale=-1.0), reads=bR, writes=[etsR])
            fw.op("act", lambda h: h.activation(out=etmp_s, in_=etmp_s, func=AF.Ln, bias=1.0), reads=[etsR], writes=[etsR])
            fw.op("act", lambda h: h.activation(out=aTs, in_=etmp_s, func=AF.Exp, scale=-1.0 / 16.0), reads=[etsR], writes=[aTsR])

            def vsl(slot):
                return 4 if slot == 5 else slot

            def o_epilogue(grv, grvR, M, bo, boR):
                ss, ssR = stat_col()
                fw.op("act", lambda h: h.activation(out=a_tm[0:M, 0:DV], in_=bo[0:M, :], func=AF.Square, accum_out=ss[0:M, :]),
                      reads=boR, writes=[a_tmR, ssR])
                rs, rsR = rstd_small(ss, ssR, M, DV)
                fw.op("dve", lambda h: h.scalar_tensor_tensor(out=grv, in0=bo[0:M, :], scalar=rs[0:M, 0:1], in1=grv,
                                                                op0=ALU.mult, op1=ALU.mult),
                      reads=boR + [rsR, grvR], writes=[grvR])

            def og_transpose(grv, grvR, M, c0, hx):
                bk, bR = take1()
                bv = bf16v(bk)
                for c4 in range(4):
                    fw.op("pe", lambda h: h.transpose(out=bv[:, c4 * 128:c4 * 128 + M], in_=grv[:, c4 * 128:(c4 + 1) * 128],
                                                      identity=ident_bf[0:M, 0:M]),
                          reads=[grvR, identbR], writes=bR, signal=(c4 == 3))
                src = bv[:, 0:512].rearrange("p (c m) -> p c m", m=128)[:, :, 0:M]
                gb = gnormT[:, 4 * hx:4 * hx + 4].unsqueeze(2).to_broadcast([128, 4, M])
                fw.op("dve", lambda h: h.tensor_tensor(out=ogT[:, 4 * hx:4 * hx + 4, c0:c0 + M], in0=src, in1=gb, op=ALU.mult),
                      reads=bR + [cpkR], writes=[ogTR])

            cur = {"gen": None, "need_safe": False}

            def pump(k, safe=False):
                for _ in range(k):
                    g_ = cur["gen"]
                    if g_ is None:
                        return
                    if cur["need_safe"] and not safe:
                        return
                    try:
                        tag = next(g_)
                    except StopIteration:
                        cur["gen"] = None
                        cur["need_safe"] = False
                        return
                    cur["need_safe"] = (tag == "final")

            def drain():
                pump(10 ** 6, safe=True)

            def sample_gen(hs, ps):
                bk7, bR7 = bank_ap(7), [bankR[7]]
                for c in range(2):
                    fw.op("pe", lambda h: h.transpose(out=bk7[0:SPP, c * 128:(c + 1) * 128], in_=ksT[:, 2 * hs + c, :], identity=identf),
                          reads=[ksTR, cpkR], writes=bR7, signal=(c == 1))
                copy_op("dve", ks_sb[ps], bk7[0:SPP, 0:256], bR7, [ksbR[ps]])
                for s in range(SPP):
                    fw.op("dve", lambda h: h.tensor_tensor(out=qmask[ps][:, :, s, :], in0=qsT[:, 2 * hs:2 * hs + 2, :],
                                                             in1=oh_row[:, s:s + 1, :].to_broadcast([128, 2, 8]), op=ALU.mult),
                          reads=[qsTR, cpkR], writes=[qmR[ps]])
                bos, bosR = bank_ap(6), [bankR[6]]
                LAG = 2
                items = [(s_, c_) for s_ in range(SPP) for c_ in range(2)]
                NI = len(items)

                def s_load(n_):
                    if n_ < NI:
                        sl_, cl_ = items[n_]
                        fw.dma("sp", Sin[n_ % 3], sgla[s0 + sl_, hs, cl_ * 128:(cl_ + 1) * 128, :], writes=[SinR[n_ % 3]])

                def mk_kmask(sk):
                    fw.op("dve", lambda h: h.tensor_scalar(out=kmask[:, sk % 2, :], in0=ks_sb[ps], scalar1=identf[0:SPP, sk:sk + 1], scalar2=None,
                                                             op0=ALU.mult),
                          reads=[ksbR[ps], cpkR], writes=[kmR[sk % 2]])
                mk_kmask(0)
                s_load(0)
                s_load(1)
                yield "item"
                for n in range(NI + LAG):
                    if n < NI:
                        s_, c_ = items[n]
                        k3 = n % 3
                        km = s_ % 2
                        s_load(n + 2)
                        if c_ == 0 and s_ + 1 < SPP:
                            mk_kmask(s_ + 1)
                        mm(bk7[:, :], kmask[:, km, c_ * 128:(c_ + 1) * 128], vs[ps], True, True, [kmR[km], vsR[ps]], bR7, True)
                        fw.op("dve", lambda h: h.scalar_tensor_tensor(out=Snew[k3], in0=Sin[k3], scalar=aTs[:, 2 * hs + c_, s_:s_ + 1], in1=bk7[:, :],
                                                                        op0=ALU.mult, op1=ALU.add),
                              reads=bR7 + [SinR[k3], aTsR], writes=[SnewR[k3]])
                        fw.dma("sp", glas_d[s0 + s_, hs, c_ * 128:(c_ + 1) * 128, :], Snew[k3], reads=[SnewR[k3]], writes=[R_glas])
                        fw.op("act", lambda h: h.copy(out=Snbf[k3], in_=Snew[k3]), reads=[SnewR[k3]], writes=[SnbfR[k3]])
                    m_ = n - LAG
                    if m_ >= 0:
                        s_, c_ = items[m_]
                        mm(bos[0:SPP, :], qmask[ps][:, c_, s_, :], Snbf[m_ % 3], m_ == 0, m_ == NI - 1, [qmR[ps], SnbfR[m_ % 3]], bosR, True)
                    yield "item"
                yield "final"
                o_epilogue(grs[ps], grsR[ps], SPP, bos, bosR)
                og_transpose(grs[ps], grsR[ps], SPP, samp_c0, hs)

            pump2 = {"n": 0}

            def pump_tm():
                pump2["n"] += 1
                if pump2["n"] % 2 == 0:
                    pump(1)

            for hh in range(NH):
                par = hh % 2
                def ev_r(slot, M, c0, bk, bR):
                    if slot == 4:
                        fw.op("act", lambda h: h.activation(out=grs[par], in_=bk[0:M, :], func=AF.Silu), reads=bR, writes=[grsR[par]])
                    else:
                        fw.op("act", lambda h: h.activation(out=gr[0:M, vsl(slot), :], in_=bk[0:M, :], func=AF.Silu),
                              reads=bR, writes=[grR[vsl(slot)]])
                tm_proj(w_in, OR + hh * 512, allsubs, lambda k, c0, M: aT[:, k, c0:c0 + M], [aTR], ev_r, pump_fn=pump_tm)
                gla_decay(hh, tsubs, glr_aug, glrR, g1s, bfm, bfmR, Ah, AhR)
                pump(1, safe=True)
                blk, bkR = next_block(wview(w_in, 0, KC, OQ + hh * 256, 256), KC, 256)
                for c in range(2):
                    Dt, DR = take2()
                    fm_group(Dt, DR, 128, lambda k: blk[:, k, c * 128:(c + 1) * 128], aT, KC, colsA, colsB, bkR + [aTR])
                    fw.op("act", lambda h: h.activation(out=Et[0][:, 0:NP], in_=bfm[:, c, 0:NP], func=AF.Exp),
                          reads=[bfmR], writes=[EtR[0]])
                    fw.op("dve", lambda h: h.scalar_tensor_tensor(out=qbT[:, c, 0:NP], in0=Dt[:, 0:NP], scalar=DK ** -0.5, in1=Et[0][:, 0:NP],
                                                                    op0=ALU.mult, op1=ALU.mult),
                          reads=DR + [EtR[0]], writes=[qbTR])
                    fw.op("dve", lambda h: h.tensor_scalar(out=qsT[:, 2 * hh + c, :], in0=Dt[:, samp_c0:samp_c0 + SPP], scalar1=DK ** -0.5,
                                                             scalar2=None, op0=ALU.mult),
                          reads=DR, writes=[qsTR])
                    pump(1, safe=True)
                blk, bkR = next_block(wview(w_in, 0, KC, OK_ + hh * 256, 256), KC, 256)
                for c in range(2):
                    Dt, DR = take2()
                    fm_group(Dt, DR, 128, lambda k: blk[:, k, c * 128:(c + 1) * 128], aT, KC, colsA, colsB, bkR + [aTR])
                    fw.op("act", lambda h: h.activation(out=Et[1][:, 0:NP], in_=bfm[:, c, 0:NP], func=AF.Exp, scale=-1.0),
                          reads=[bfmR], writes=[EtR[1]])
                    fw.op("dve", lambda h: h.tensor_tensor(out=kbT[:, c, 0:NP], in0=Dt[:, 0:NP], in1=Et[1][:, 0:NP], op=ALU.mult),
                          reads=DR + [EtR[1]], writes=[kbTR])
                    for (slot, M, cc) in tsubs:
                        fw.op("act", lambda h: h.activation(out=Et[0][:, cc:cc + M], in_=bfm[:, c, cc:cc + M], func=AF.Exp, scale=-1.0,
                                                              bias=bfm[:, c, cc + M - 1:cc + M]),
                              reads=[bfmR], writes=[EtR[0]])
                    fw.op("dve", lambda h: h.tensor_tensor(out=kdT[:, c, 0:NP], in0=Dt[:, 0:NP], in1=Et[0][:, 0:NP], op=ALU.mult),
                          reads=DR + [EtR[0]], writes=[kdTR])
                    copy_op("dve", ksT[:, 2 * hh + c, :], Dt[:, samp_c0:samp_c0 + SPP], DR, [ksTR])
                    pump(1, safe=True)
                def ev_v(slot, M, c0, bk, bR):
                    if slot == 4:
                        copy_op(alt_eng(), vs[par], bk[0:M, :], bR, [vsR[par]])
                    else:
                        copy_op(alt_eng(), vh[0:M, vsl(slot), :], bk[0:M, :], bR, [vR])
                tm_proj(w_in, OV + hh * 512, allsubs, lambda k, c0, M: aT[:, k, c0:c0 + M], [aTR], ev_v, pump_fn=pump_tm)
                drain()
                cur["gen"] = sample_gen(hh, par)

                def stageA(ti):
                    slot, M, c0 = tsubs[ti]
                    b2 = ti % 2
                    bk, bR = take1()
                    for c in range(2):
                        mm(bk[0:M, 0:M], kbT[:, c, c0:c0 + M], qbT[:, c, c0:c0 + M], c == 0, c == 1, [kbTR, qbTR], bR, c == 1)
                    fw.op("dve", lambda h: h.tensor_tensor(out=att_sb[b2][0:M, 0:M], in0=bk[0:M, 0:M], in1=mask01[0:M, 0:M], op=ALU.mult),
                          reads=bR + [cpkR], writes=[attR[b2]])
                    kd_transpose(M, c0, kdT, kdTR, kd_tm[b2], kdR[b2])

                NT_ = len(tsubs)
                stageA(0)
                for ti, (slot, M, c0) in enumerate(tsubs):
                    b2 = ti % 2
                    vi = vsl(slot)
                    if ti + 1 < NT_:
                        stageA(ti + 1)
                    kvb = [take1() for _ in range(2)]
                    for c in range(2):
                        mm(kvb[c][0][:, :], kd_tm[b2][0:M, c * 128:(c + 1) * 128], vh[0:M, vi, :], True, True, [kdR[b2], vR], kvb[c][1], True)
                    bo, boR = take1()
                    mm(bo[0:M, :], att_sb[b2][0:M, 0:M], vh[0:M, vi, :], True, False, [attR[b2], vR], boR, False)
                    for c in range(2):
                        mm(bo[0:M, :], qbT[:, c, c0:c0 + M], Sbf[:, 2 * hh + c, :], False, c == 1, [qbTR, SbfR[2 * hh + c]], boR, c == 1)
                    for c in range(2):
                        Sv = S_[:, 2 * hh + c, :]
                        fw.op("dve", lambda h: h.scalar_tensor_tensor(out=Sv, in0=Sv, scalar=Ah[:, c, ti:ti + 1], in1=kvb[c][0][:, :],
                                                                        op0=ALU.mult, op1=ALU.add),
                              reads=kvb[c][1] + [AhR, SR[2 * hh + c]], writes=[SR[2 * hh + c]])
                        fw.op("act", lambda h: h.copy(out=Sbf[:, 2 * hh + c, :], in_=Sv),
                              reads=[SR[2 * hh + c]], writes=[SbfR[2 * hh + c]])
                    o_epilogue(gr[0:M, vi, :], grR[vi], M, bo, boR)
                    if ti >= 1:
                        sl_, Mp_, cp_ = tsubs[ti - 1]
                        og_transpose(gr[0:Mp_, vsl(sl_), :], grR[vsl(sl_)], Mp_, cp_, hh)
                    pump(1, safe=True)
                sl_, Mp_, cp_ = tsubs[NT_ - 1]
                og_transpose(gr[0:Mp_, vsl(sl_), :], grR[vsl(sl_)], Mp_, cp_, hh)
            drain()
            if p == 1:
                fw.dma("sp", glap_d.rearrange("h (c p) v -> p (h c) v", p=128), S_, reads=SR, writes=[R_glap])

            dbg("ogT", ogT, [ogTR], p)
            fw.new_phase()
            ar.reset(z1a)
            pmwT = ar.alloc([8, 552], BF16); pmwTR = fw.res("pmwT")
            z1 = ar.mark()
            XL = 15 + 512
            X = [ar.alloc([2, XL], F32) for _ in range(3)]; XR = [fw.res(f"X{i}") for i in range(3)]
            XH = [ar.alloc([2, 15 + HALO], F32) for _ in range(3)]; XHR = [fw.res(f"XH{i}") for i in range(3)]
            pmT = ar.alloc([2, 552], BF16); pmTR = fw.res("pmT")
            us = ar.alloc([8, 8], F32); usR = fw.res("us")
            tsm = ar.alloc([8], F32); tsmR = fw.res("tsm")
            sp_tm = ar.alloc([1024], F32, parts=120); sptR = fw.res("sp_tm")
            selw = ar.alloc([32], F32, parts=120); selR = fw.res("selw")
            urow = ar.alloc([1024], F32, parts=23); urowR = fw.res("urow")
            fw.dma("sp", sp_tm, spool[s0:s0 + SPP].rearrange("s r c -> (s r) c"), writes=[sptR])
            fw.dma("sp", selw, selw_d, writes=[selR])
            if p == 0:
                fw.op("dve", lambda h: h.memset(XH[0], 0.0), writes=[XHR[0]])

            def windows(Xb, XbR, L, g):
                src = 0
                sh = 1
                for stp in range(g + 1):
                    dst = 1 if src != 1 else 2
                    eng = "dve"
                    lo = 2 * sh - 1
                    fw.op(eng, lambda h: h.tensor_tensor(out=Xb[dst][:, :, lo:L], in0=Xb[src][:, :, lo:L], in1=Xb[src][:, :, lo - sh:L - sh], op=ALU.add),
                          reads=[XbR[src]], writes=[XbR[dst]])
                    src = dst
                    sh *= 2
                return src

            uD = {}

            def u_mm(g):
                blk, bkR = next_block(wview(w_in, 0, KC, OU + g * 256, 256), KC, 256)
                for c in range(2):
                    Dt, DR = Dps[c], [bankR[2 * c], bankR[2 * c + 1]]
                    fm_group(Dt, DR, 128, lambda k: blk[:, k, c * 128:(c + 1) * 128], aT, KC, colsA, colsB, bkR + [aTR])
                    uD[(g, c)] = (Dt, DR)

            def u_evac(g):
                for c in range(2):
                    Dt, DR = uD[(g, c)]
                    copy_op("act", X[0][:, c, 15:XL], Dt[:, 0:512], DR, [XR[0]])
                    copy_op("dve", us[:, 2 * g + c, :], Dt[:, samp_c0:samp_c0 + SPP], DR, [usR])
                    if p == 0:
                        copy_op("dve", XH[0][:, c, 15:15 + HALO], Dt[:, 512:512 + HALO], DR, [XHR[0]])
                        copy_op("dve", X[0][:, c, 0:15], Dt[:, 512 + HALO - 15:512 + HALO], DR, [XR[0]])
                    else:
                        copy_op("dve", X[0][:, c, 0:15], uhist[:, 2 * g + c, :], [uhistR], [XR[0]])
                    copy_op("dve", uhist[:, 2 * g + c, :], X[0][:, c, XL - 15:XL], [XR[0]], [uhistR])

            u_mm(0)
            u_evac(0)
            for g in range(4):
                wd = 2 ** (g + 1)
                if g + 1 < 4:
                    u_mm(g + 1)
                f = windows(X, XR, XL, g)
                if p == 0:
                    fw.op("dve", lambda h: h.tensor_tensor(out=X[f][:, :, 15:31], in0=X[f][:, :, 15:31],
                                                             in1=corr[:, g:g + 1, :].to_broadcast([128, 2, 16]), op=ALU.mult),
                          reads=[XR[f], cpkR], writes=[XR[f]])
                fw.op("dve", lambda h: h.scalar_tensor_tensor(out=pmT[:, :, 0:512], in0=X[f][:, :, 15:XL], scalar=1.0 / wd, in1=X[0][:, :, 15:XL],
                                                                op0=ALU.mult, op1=ALU.subtract),
                      reads=[XR[f], XR[0]], writes=[pmTR])
                if p == 0:
                    fh = windows(XH, XHR, 15 + HALO, g)
                    fw.op("dve", lambda h: h.scalar_tensor_tensor(out=pmT[:, :, 512:512 + HALO], in0=XH[fh][:, :, 15:15 + HALO], scalar=1.0 / wd,
                                                                    in1=XH[0][:, :, 15:15 + HALO], op0=ALU.mult, op1=ALU.subtract),
                          reads=[XHR[fh], XHR[0]], writes=[pmTR])
                for c in range(2):
                    cc = 2 * g + c
                    bk, bR = bank_ap(6 + c), [bankR[6 + c]]
                    mm(bk[:, 0:SPP], sp_tm[:, cc * 128:(cc + 1) * 128], selw[:, g * 8:(g + 1) * 8], True, True, [sptR, selR], bR, True)
                    fw.op("dve", lambda h: h.tensor_scalar(out=tsm, in0=us[:, cc, :], scalar1=1.0 / wd - 1.0, scalar2=None, op0=ALU.mult),
                          reads=[usR], writes=[tsmR])
                    fw.op("dve", lambda h: h.scalar_tensor_tensor(out=pmT[:, c, samp_c0:samp_c0 + SPP], in0=bk[:, 0:SPP], scalar=1.0 / wd, in1=tsm,
                                                                    op0=ALU.mult, op1=ALU.add),
                          reads=bR + [tsmR], writes=[pmTR])
                blk, bkR = next_block(w_pool[g].rearrange("(k p) n -> p k n", p=128), 2, 256)
                for dc in range(2):
                    Dt, DR = Dps[2], [bankR[4], bankR[5]]
                    fm_group(Dt, DR, 128, lambda k: blk[:, k, dc * 128:(dc + 1) * 128], pmT, 2, colsA, colsB, bkR + [pmTR])
                    fw.op("act", lambda h: h.activation(out=pmwT[:, 2 * g + dc, 0:NC], in_=Dt[:, 0:NC], func=AF.Copy, scale=pscale[:, 2 * g + dc:2 * g + dc + 1]),
                          reads=DR + [cpkR], writes=[pmwTR])
                if g + 1 < 4:
                    u_evac(g + 1)
            st["pp"] = 0
            Dt, DR = take2()
            for cc in range(8):
                fw.op("pe", lambda h: h.transpose(out=Dt[0:SPP, cc * 128:(cc + 1) * 128], in_=us[:, cc, :], identity=identf),
                      reads=[usR, cpkR], writes=DR, signal=(cc == 7))
            copy_op("act", urow[0:SPP, :], Dt[0:SPP, :], DR, [urowR])
            fw.dma("sp", pools_d[s0:s0 + SPP, 14, :], urow[0:SPP, :], reads=[urowR], writes=[R_pools])
            fw.dma("sp", pools_d[s0:s0 + SPP, 0:14, :], spool[s0:s0 + SPP, 1:15, :], writes=[R_pools])
            if p == 1:
                Dt, DR = take2()
                for cc in range(8):
                    fw.op("pe", lambda h: h.transpose(out=Dt[0:15, cc * 128:(cc + 1) * 128], in_=uhist[:, cc, :], identity=identf),
                          reads=[uhistR, cpkR], writes=DR, signal=(cc == 7))
                copy_op("act", urow[0:15, :], Dt[0:15, :], DR, [urowR])
                fw.dma("sp", poolp_d, urow[0:15, :], reads=[urowR], writes=[R_poolp])

            dbg("pmwT", pmwT, [pmwTR], p)
            fw.new_phase()
            ar.reset(z1)
            mT = ar.alloc([KC, 552], BF16); mTR = fw.res("mT")
            sga = ar.alloc([4, 552], F32); sgaR = [fw.res(f"sga{i}") for i in range(4)]
            m1 = ar.alloc([4, 552], F32); m1R = [fw.res(f"m1{i}") for i in range(4)]
            for hb in range(8):
                blk, bkR = next_block(wview(w_in, 0, KC, OGA + hb * 256, 256), KC, 256)
                for n in range(2):
                    Dt, DR = take2()
                    fm_group(Dt, DR, 128, lambda k: blk[:, k, n * 128:(n + 1) * 128], aT, KC, colsA, colsB, bkR + [aTR])
                    fw.op("act", lambda h: h.activation(out=sga[:, n, 0:NC], in_=Dt[:, 0:NC], func=AF.Sigmoid), reads=DR, writes=[sgaR[n]])
                blk, bkR = next_block(wview(w_a, 0, KC, hb * 256, 256), KC, 256)
                for n in range(2):
                    Dt, DR = take2()
                    fm_group(Dt, DR, 128, lambda k: blk[:, k, n * 128:(n + 1) * 128], ogT, KC, colsA, colsB, bkR + [ogTR])
                    fw.op("dve", lambda h: h.tensor_tensor(out=m1[:, n, 0:NC], in0=Dt[:, 0:NC], in1=sga[:, n, 0:NC], op=ALU.mult),
                          reads=DR + [sgaR[n]], writes=[m1R[n]])
                blk, bkR = next_block(wview(w_in, 0, KC, OGB + hb * 256, 256), KC, 256)
                for n in range(2):
                    Dt, DR = take2()
                    fm_group(Dt, DR, 128, lambda k: blk[:, k, n * 128:(n + 1) * 128], aT, KC, colsA, colsB, bkR + [aTR])
                    fw.op("act", lambda h: h.activation(out=sga[:, 2 + n, 0:NC], in_=Dt[:, 0:NC], func=AF.Sigmoid), reads=DR, writes=[sgaR[2 + n]])
                blk, bkR = next_block(wview(w_b, 0, 8, hb * 256, 256), 8, 256)
                for n in range(2):
                    Dt, DR = take2()
                    fm_group(Dt, DR, 128, lambda k: blk[:, k, n * 128:(n + 1) * 128], pmwT, 8, colsA, colsB, bkR + [pmwTR])
                    fw.op("dve", lambda h: h.tensor_tensor(out=sga[:, 2 + n, 0:NC], in0=Dt[:, 0:NC], in1=sga[:, 2 + n, 0:NC], op=ALU.mult),
                          reads=DR + [sgaR[2 + n]], writes=[sgaR[2 + n]])
                    fw.op("dve", lambda h: h.tensor_tensor(out=mT[:, hb * 2 + n, 0:NC], in0=sga[:, 2 + n, 0:NC], in1=m1[:, n, 0:NC], op=ALU.add),
                          reads=[sgaR[2 + n], m1R[n]], writes=[mTR])
            dbg("mT", mT, [mTR], p)
            for nb in range(4):
                def ev_out(slot, M, c0, bk, bR):
                    hv = hT[0:M, slot, nb * 512:(nb + 1) * 512]
                    fw.op("dve", lambda h: h.tensor_tensor(out=hv, in0=hv, in1=bk[0:M, :], op=ALU.add), reads=bR + [hR[slot]], writes=[hR[slot]])
                tm_proj(w_out, nb * 512, allsubs, lambda k, c0, M: mT[:, k, c0:c0 + M], [mTR], ev_out)

            dbg("h1", hT, hR, p)
            norm_to_aT(allsubs, 1)
            dbg("cT", aT, [aTR], p)
            fw.new_phase()
            ar.reset(zmark)
            actT = ar.alloc([FC, 520], BF16); actTR = fw.res("actT")
            gext = [ar.alloc([514], F32) for _ in range(2)]; gextR = [fw.res(f"gext{i}") for i in range(2)]
            yt = [ar.alloc([512], F32) for _ in range(4)]; ytR = [fw.res(f"yt{i}") for i in range(4)]
            us_raw = ar.alloc([FC, 8], F32); usrR = fw.res("us_raw")
            stT = ar.alloc([FC, 16], F32); stTR = fw.res("stT")
            sct = [ar.alloc([512], F32, parts=16) for _ in range(2)]; sctR = [fw.res(f"sct{i}") for i in range(2)]
            grow = [ar.alloc([512], F32, parts=10) for _ in range(2)]; growR = [fw.res(f"grow{i}") for i in range(2)]
            tA = ar.alloc([FC, 8], F32); tAR = fw.res("tA")
            tB = ar.alloc([FC, 8], F32); tBR = fw.res("tB")
            if p == 0:
                gB = (542, 552)
            else:
                gB = (512, 520)
            uB = (samp_c0, samp_c0 + SPP)
            sconv_v = sconv[s0:s0 + SPP].rearrange("s r f -> (s r) f")
            NBLK = 11

            def sconv_load(j):
                ncb_ = 4 if j < 10 else 3
                fw.dma("sp", sct[j % 2][:, 0:ncb_ * 128], sconv_v[:, j * 512:j * 512 + ncb_ * 128], writes=[sctR[j % 2]])

            def sconv_tr(j):
                ncb_ = 4 if j < 10 else 3
                k2_ = j % 2
                bk, bR = bank_ap(7), [bankR[7]]
                for n_ in range(ncb_):
                    fw.op("pe", lambda h: h.transpose(out=bk[:, n_ * 16:(n_ + 1) * 16], in_=sct[k2_][:, n_ * 128:(n_ + 1) * 128], identity=identf[0:16, 0:16]),
                          reads=[sctR[k2_], cpkR], writes=bR, signal=(n_ == ncb_ - 1))
                copy_op("act", stT[:, 4 * j:4 * j + ncb_, :], bk[:, 0:ncb_ * 16].rearrange("p (n s) -> p n s", s=16), bR, [stTR])
            for i in range(22):
                ncb = 2 if i < 21 else 1
                if i % 2 == 0 and i // 2 < NBLK:
                    sconv_load(i // 2)
                if i % 2 == 1 and i // 2 < NBLK:
                    sconv_tr(i // 2)
                blkg, bgR = next_block(wview(w_up, 0, KC, i * 256, ncb * 128), KC, ncb * 128)
                for n in range(ncb):
                    f = 2 * i + n
                    e2 = f % 2
                    y4 = f % 4
                    Dg, DgR = take2()
                    fm_group(Dg, DgR, 128, lambda k: blkg[:, k, n * 128:(n + 1) * 128], aT, KC, colsA, gB, bgR + [aTR])
                    copy_op("act", gext[e2][:, 2:514], Dg[:, 0:512], DgR, [gextR[e2]])
                    if p == 0:
                        copy_op("dve", gext[e2][:, 0:2], Dg[:, 512:514], DgR, [gextR[e2]])
                        copy_op("dve", graw10[:, f, 2:10], Dg[:, 514:522], DgR, [graw10R])
                    else:
                        copy_op("dve", gext[e2][:, 0:2], ghist[:, f, :], [ghistR], [gextR[e2]])
                        copy_op("dve", graw10[:, f, 2:10], Dg[:, 512:520], DgR, [graw10R])
                    copy_op("dve", ghist[:, f, :], Dg[:, 510:512], DgR, [ghistR])
                    fw.op("act", lambda h: h.activation(out=yt[y4], in_=gext[e2][:, 2:514], func=AF.Identity, scale=convw[:, 2, f:f + 1], bias=convw[:, 3, f:f + 1]),
                          reads=[gextR[e2], cpkR], writes=[ytR[y4]])
                    fw.op("dve", lambda h: h.scalar_tensor_tensor(out=yt[y4], in0=gext[e2][:, 1:513], scalar=convw[:, 1, f:f + 1], in1=yt[y4],
                                                                    op0=ALU.mult, op1=ALU.add),
                          reads=[gextR[e2], ytR[y4], cpkR], writes=[ytR[y4]])
                    fw.op("dve", lambda h: h.scalar_tensor_tensor(out=yt[y4], in0=gext[e2][:, 0:512], scalar=convw[:, 0, f:f + 1], in1=yt[y4],
                                                                    op0=ALU.mult, op1=ALU.add),
                          reads=[gextR[e2], ytR[y4], cpkR], writes=[ytR[y4]])
                    fw.op("act", lambda h: h.activation(out=yt[y4], in_=yt[y4], func=AF.Silu), reads=[ytR[y4]], writes=[ytR[y4]])
                blku, buR = next_block(wview(w_up, 0, KC, DFF + i * 256, ncb * 128), KC, ncb * 128)
                for n in range(ncb):
                    f = 2 * i + n
                    y4 = f % 4
                    Du, DuR = take2()
                    fm_group(Du, DuR, 128, lambda k: blku[:, k, n * 128:(n + 1) * 128], aT, KC, colsA, uB, buR + [aTR])
                    copy_op("dve", us_raw[:, f, :], Du[:, 512:520], DuR, [usrR])
                    fw.op("dve", lambda h: h.tensor_tensor(out=actT[:, f, 0:512], in0=yt[y4], in1=Du[:, 0:512], op=ALU.mult),
                          reads=[ytR[y4]] + DuR, writes=[actTR])
            copy_op("dve", graw10[:, :, 0:2], ghist, [ghistR], [graw10R])
            stv = stT.rearrange("p f (s r) -> p f s r", r=2)
            gsr = graw10[:, :, 2:10]

            def bc(j):
                return convw[:, j, :].unsqueeze(2).to_broadcast([128, FC, 8])
            fw.op("dve", lambda h: h.tensor_tensor(out=tA, in0=gsr, in1=bc(2), op=ALU.mult), reads=[graw10R, cpkR], writes=[tAR])
            fw.op("dve", lambda h: h.tensor_tensor(out=tA, in0=tA, in1=bc(3), op=ALU.add), reads=[tAR, cpkR], writes=[tAR])
            fw.op("dve", lambda h: h.tensor_tensor(out=tB, in0=stv[:, :, :, 1], in1=bc(1), op=ALU.mult), reads=[stTR, cpkR], writes=[tBR])
            fw.op("dve", lambda h: h.tensor_tensor(out=tA, in0=tA, in1=tB, op=ALU.add), reads=[tAR, tBR], writes=[tAR])
            fw.op("dve", lambda h: h.tensor_tensor(out=tB, in0=stv[:, :, :, 0], in1=bc(0), op=ALU.mult), reads=[stTR, cpkR, tAR], writes=[tBR])
            fw.op("dve", lambda h: h.tensor_tensor(out=tA, in0=tA, in1=tB, op=ALU.add), reads=[tAR, tBR], writes=[tAR])
            fw.op("act", lambda h: h.activation(out=tA, in_=tA, func=AF.Silu), reads=[tAR], writes=[tAR])
            fw.op("dve", lambda h: h.tensor_tensor(out=actT[:, :, 512:520], in0=tA, in1=us_raw, op=ALU.mult), reads=[tAR, usrR], writes=[actTR])
            for i in range(NBLK):
                ncb = 4 if i < 10 else 3
                k2 = i % 2
                bk, bR = take1()
                for n in range(ncb):
                    fw.op("pe", lambda h: h.transpose(out=bk[0:10, n * 128:(n + 1) * 128], in_=graw10[:, 4 * i + n, :], identity=identf),
                          reads=[graw10R, cpkR], writes=bR, signal=(n == ncb - 1))
                copy_op(alt_eng(), grow[k2][:, 0:ncb * 128], bk[0:10, 0:ncb * 128], bR, [growR[k2]])
                fw.dma("sp", convs_d[s0:s0 + SPP, 1, i * 512:i * 512 + ncb * 128], grow[k2][2:10, 0:ncb * 128], reads=[growR[k2]], writes=[R_convs])
                if p == 1:
                    fw.dma("sp", convp_d[:, i * 512:i * 512 + ncb * 128], grow[k2][0:2, 0:ncb * 128], reads=[growR[k2]], writes=[R_convp])
            fw.dma("sp", convs_d[s0:s0 + SPP, 0, :], sconv[s0:s0 + SPP, 1, :], writes=[R_convs])
            dbg("actT", actT, [actTR], p)
            franges = [(0, 8), (8, 16), (16, 24), (24, 32), (32, 40), (40, 43)]
            for nb in range(4):
                banks = [take1() for _ in subs5]
                for (f0, f1) in franges:
                    nf = f1 - f0
                    blk, bkR = next_block(wview(w_down, f0 * 128, nf, nb * 512, 512), nf, 512)
                    for si, (slot, M, c0) in enumerate(subs5):
                        bk, bR = banks[si]
                        for fi in range(nf):
                            f = f0 + fi
                            mm(bk[0:M, :], actT[:, f, (512 if slot == 4 else c0):(512 if slot == 4 else c0) + M], blk[:, fi, :], f == 0, f == FC - 1, bkR + [actTR], bR, fi == nf - 1)
                for si, (slot, M, c0) in enumerate(subs5):
                    bk, bR = banks[si]
                    hv = hT[0:M, slot, nb * 512:(nb + 1) * 512]
                    fw.op("dve", lambda h: h.tensor_tensor(out=hv, in0=hv, in1=bk[0:M, :], op=ALU.add), reads=bR + [hR[slot]], writes=[hR[slot]])

            dbg("h2", hT, hR, p)
            norm_to_aT(subs5, 2)
            fw.new_phase()
            ar.reset(zmark)
            plT = ar.alloc([2, 520], BF16); plTR = fw.res("plT")
            pl_tm = ar.alloc([5, PLE], F32); plR = [fw.res(f"pl{i}") for i in range(5)]
            sg = ar.alloc([5, 512], F32); sgR = [fw.res(f"sg{i}") for i in range(5)]
            gfin = ar.alloc([D], F32); gfinR = fw.res("gfin")
            ybuf = [ar.alloc([D], F32) for _ in range(2)]; ybufR = [fw.res(f"ybuf{i}") for i in range(2)]
            fw.dma("sp", gfin, gfin_d.partition_broadcast(128), writes=[gfinR])
            for (slot, M, c0) in mains:
                fw.dma("sp", pl_tm[:, slot, :], pmd[p * 512 + slot * 128:p * 512 + slot * 128 + 128, :], writes=[plR[slot]])
            fw.dma("sp", pl_tm[0:SPP, 4, :], psd[s0:s0 + SPP, :], writes=[plR[4]])
            for (slot, M, c0) in subs5:
                bk, bR = take1()
                for c2 in range(2):
                    fw.op("pe", lambda h: h.transpose(out=bk[:, c2 * 128:c2 * 128 + M], in_=pl_tm[0:M, slot, c2 * 128:(c2 + 1) * 128],
                                                      identity=identf[0:M, 0:M]),
                          reads=[plR[slot], cpkR], writes=bR, signal=(c2 == 1))
                src = bk[:, 0:256].rearrange("p (c m) -> p c m", m=128)[:, :, 0:M]
                pc0 = 512 if slot == 4 else c0
                copy_op("act", plT[:, :, pc0:pc0 + M], src, bR, [plTR])
            for nb in range(4):
                tm_proj(w_pg, nb * 512, subs5, lambda k, c0, M: aT[:, k, c0:c0 + M], [aTR],
                        lambda slot, M, c0, bk, bR: fw.op("act", lambda h: h.activation(out=sg[0:M, slot, :], in_=bk[0:M, :], func=AF.Sigmoid),
                                                          reads=bR, writes=[sgR[slot]]))
                blk, bkR = next_block(wview(w_ple, 0, 2, nb * 512, 512), 2, 512)
                for (slot, M, c0) in subs5:
                    bk, bR = take1()
                    pc0 = 512 if slot == 4 else c0
                    tm_group(bk, bR, M, lambda k: plT[:, k, pc0:pc0 + M], lambda k: blk[:, k, :], 2, 512, bkR + [plTR])
                    fw.op("dve", lambda h: h.tensor_tensor(out=sg[0:M, slot, :], in0=bk[0:M, :], in1=sg[0:M, slot, :], op=ALU.mult),
                          reads=bR + [sgR[slot]], writes=[sgR[slot]])
                    hv = hT[0:M, slot, nb * 512:(nb + 1) * 512]
                    fw.op("dve", lambda h: h.tensor_tensor(out=hv, in0=hv, in1=sg[0:M, slot, :], op=ALU.add),
                          reads=[sgR[slot], hR[slot]], writes=[hR[slot]])
            dbg("h3", hT, hR, p)
            for si, (slot, M, c0) in enumerate(subs5):
                k2 = si % 2
                ss, ssR = stat_col()
                fw.op("act", lambda h: h.activation(out=ybuf[k2][0:M, :], in_=hT[0:M, slot, :], func=AF.Square, accum_out=ss[0:M, :]),
                      reads=[hR[slot]], writes=[ybufR[k2], ssR])
                rs, rsR = rstd_small(ss, ssR, M, D)
                fw.op("dve", lambda h: h.scalar_tensor_tensor(out=ybuf[k2][0:M, :], in0=hT[0:M, slot, :], scalar=rs[0:M, 0:1], in1=gfin[0:M, :],
                                                                op0=ALU.mult, op1=ALU.mult),
                      reads=[hR[slot], rsR, gfinR], writes=[ybufR[k2]])
                if slot < 4:
                    fw.dma("sp", y_d[p * 512 + slot * 128:p * 512 + slot * 128 + 128, :], ybuf[k2], reads=[ybufR[k2]], writes=[R_y])
                else:
                    fw.dma("sp", ys_d[s0:s0 + SPP, :], ybuf[k2][0:SPP, :], reads=[ybufR[k2]], writes=[R_ys])
                if p == 0:
                    if slot < 4:
                        fw.dma("sp", hT[:, slot, :], xm[512 + slot * 128:512 + slot * 128 + 128, :], writes=[hR[slot]])
                    else:
                        fw.dma("sp", hT[0:SPP, 4, :], xs[SPP:2 * SPP, :], writes=[hR[4]])

        prefix_pass(0)
        prefix_pass(1)
        main_pass(0)
        main_pass(1)
        fw.finish(outs_res + dbg_list)

    fw.dry = True
    emit()
    fw.dry = False
    emit()
    return nc, fw


_CACHE = {}


def _host_consts(half):
    cpk = np.zeros((128, C_TOT), np.float32)
    cpk[:, C_ID:C_ID + 128] = np.eye(128, dtype=np.float32)
    s = np.arange(128)[:, None]
    t = np.arange(128)[None, :]
    cpk[:, C_UC:C_UC + 128] = np.where(s <= t, -1.0 / 16.0, 0.0)
    cpk[:, C_MK:C_MK + 128] = np.where(s <= t, 1.0, 0.0)
    cpk[:, C_OH:C_OH + 64] = np.eye(8, dtype=np.float32).reshape(1, 64)
    corr = np.ones((4, 16), np.float32)
    if half == 0:
        for g in range(4):
            wd = 2 ** (g + 1)
            for tt in range(16):
                corr[g, tt] = wd / min(tt + 1, wd)
    cpk[:, C_CORR:C_CORR + 64] = corr.reshape(1, 64)
    return cpk


def kernel(x_prompt, x_sample, state_gla, state_pool, state_conv, p_prompt, p_sample,
           norm_mix, w_in, w_gate_up, b_gate, gla_norm, w_branch_a, w_pool, pool_scale,
           w_branch_b, w_out, norm_ffn, w_up, conv_w, conv_b, w_down, norm_ple, w_ple_gate,
           w_ple, norm_final):
    f = lambda a: np.ascontiguousarray(np.asarray(a, dtype=np.float32))
    x_prompt, x_sample = f(x_prompt), f(x_sample)
    state_gla, state_pool, state_conv = f(state_gla), f(state_pool), f(state_conv)
    p_prompt, p_sample = f(p_prompt), f(p_sample)
    if "nc" not in _CACHE:
        _CACHE["nc"] = build_program()
    nc, fw = _CACHE["nc"]

    def fm(v, nch):
        return f(v).reshape(nch, 128).T
    base = {}
    for half in (0, 1):
        cpk = _host_consts(half)
        cpk[:, C_GT + 0:C_GT + 16] = fm(norm_mix[0], 16)
        cpk[:, C_GT + 16:C_GT + 32] = fm(norm_ffn[0], 16)
        cpk[:, C_GT + 32:C_GT + 48] = fm(norm_ple[0], 16)
        cpk[:, C_GN:C_GN + 16] = fm(gla_norm[0], 16)
        cpk[:, C_PS:C_PS + 8] = fm(pool_scale[0], 8)
        for j in range(3):
            cpk[:, C_CW + j * FC:C_CW + (j + 1) * FC] = fm(conv_w[0, j], FC)
        cpk[:, C_CW + 3 * FC:C_CW + 4 * FC] = fm(conv_b[0], FC)
        base[half] = cpk
    selw = np.zeros((120, 32), np.float32)
    for g in range(4):
        wd = 2 ** (g + 1)
        for s in range(8):
            for r in range(15 - (wd - 1), 15):
                selw[s * 15 + r, g * 8 + s] = 1.0
    wg_aug = np.concatenate([f(w_gate_up[0]), f(b_gate[0])[None, :]], axis=0)
    shared = {
        "w_in": f(w_in[0]), "wg_aug": wg_aug, "w_a": f(w_branch_a[0]), "w_pool": f(w_pool[0]),
        "w_b": f(w_branch_b[0]), "w_out": f(w_out[0]), "w_up": f(w_up[0]), "w_down": f(w_down[0]),
        "w_pg": f(w_ple_gate[0]), "w_ple": f(w_ple[0]), "gfin": f(norm_final), "selw": selw,
    }
    in_maps = []
    for c in range(8):
        b, half = c // 2, c % 2
        m = dict(shared)
        m["cpk"] = base[half]
        m["xm"] = x_prompt[b, half * 1024:(half + 1) * 1024]
        if half == 1:
            m["xpre"] = x_prompt[b, 0:NPRE]
            m["xh"] = x_prompt[b, NPRE:1024]
        else:
            m["xpre"] = np.zeros((NPRE, D), np.float32)
            m["xh"] = np.zeros((HALO, D), np.float32)
        m["xs"] = x_sample[c * 16:(c + 1) * 16, 0]
        m["pm"] = p_prompt[0, b, half * 1024:(half + 1) * 1024]
        m["ps"] = p_sample[0, c * 16:(c + 1) * 16, 0]
        m["sgla"] = state_gla[0, c * 16:(c + 1) * 16]
        m["spool"] = state_pool[0, c * 16:(c + 1) * 16]
        m["sconv"] = state_conv[0, c * 16:(c + 1) * 16]
        in_maps.append({k: np.ascontiguousarray(v) for k, v in m.items()})
    res = run_bass_kernel_spmd(nc, in_maps, core_ids=list(range(8)))
    R = res.results
    if DEBUG:
        _CACHE["raw"] = R
    B = x_prompt.shape[0]
    y_prompt = np.zeros((B, 2048, D), np.float32)
    y_sample = np.zeros((128, 1, D), np.float32)
    gla_p = np.zeros((1, B, NH, DK, DV), np.float32)
    pool_p = np.zeros((1, B, 15, 1024), np.float32)
    conv_p = np.zeros((1, B, 2, DFF), np.float32)
    gla_s = np.zeros((1, 128, NH, DK, DV), np.float32)
    pool_s = np.zeros((1, 128, 15, 1024), np.float32)
    conv_s = np.zeros((1, 128, 2, DFF), np.float32)
    for c in range(8):
        b, half = c // 2, c % 2
        r = R[c]
        y_prompt[b, half * 1024:(half + 1) * 1024] = r["y"]
        y_sample[c * 16:(c + 1) * 16, 0] = r["ys"]
        gla_s[0, c * 16:(c + 1) * 16] = r["gla_s"]
        pool_s[0, c * 16:(c + 1) * 16] = r["pool_s"]
        conv_s[0, c * 16:(c + 1) * 16] = r["conv_s"]
        if half == 1:
            gla_p[0, b] = r["gla_p"]
            pool_p[0, b] = r["pool_p"]
            conv_p[0, b] = r["conv_p"]
    return (y_prompt, y_sample, gla_p, pool_p, conv_p, gla_s, pool_s, conv_s)
```
